# Optimizing a Trainium2 kernel written in Bass

```python
import jax
import jax.numpy as jnp
from jax import lax
import numpy as np

D_MODEL = 1024
BATCH = 8
SEQ = 2048
DEPTH = 4

HEAD_DIM = 64
ATT_HEADS = D_MODEL // 128
ATT_WIDTH = ATT_HEADS * HEAD_DIM
MOBA_BLOCK = 256
MOBA_TOPK = 3
Q_CHUNK = 32
ROPE_THETA = 10000.0
CONF_WIDTH = D_MODEL // 4
CONF_KERNEL = 31
POOL_GROUPS = 4
POOL_WIDTH = D_MODEL // 4
POOL_GW = POOL_WIDTH // POOL_GROUPS
POOL_WINDOWS = (2, 4, 8, 16)
SC_WIDTH = D_MODEL // 4
SC_KERNEL = 3
N_BRANCH = 4
D_FF = 4 * D_MODEL
IN_WIDTHS = (ATT_WIDTH, ATT_WIDTH, ATT_WIDTH, 2 * CONF_WIDTH, POOL_WIDTH,
             SC_WIDTH, SC_WIDTH, SC_WIDTH, N_BRANCH * D_MODEL)
IN_WIDTH = sum(IN_WIDTHS)
SPLIT_POINTS = tuple(int(s) for s in np.cumsum(IN_WIDTHS)[:-1])
DEEPNORM_ALPHA = (2.0 * DEPTH) ** 0.25
DEEPNORM_BETA = (8.0 * DEPTH) ** -0.25
LN_EPS = 1e-5

kernel_name = 'hybrid_gated_moba_conv_pool_shortconv_trunk'


def layer_norm(x, g, b):
    xf = x.astype(jnp.float32)
    mu = jnp.mean(xf, axis=-1, keepdims=True)
    var = jnp.mean(jnp.square(xf - mu), axis=-1, keepdims=True)
    y = (xf - mu) * lax.rsqrt(var + LN_EPS)
    return (y * g.astype(jnp.float32) + b.astype(jnp.float32)).astype(x.dtype)


def rope_tables(T):
    inv = ROPE_THETA ** (-jnp.arange(0, HEAD_DIM, 2, dtype=jnp.float32) / HEAD_DIM)
    ang = jnp.arange(T, dtype=jnp.float32)[:, None] * inv[None, :]
    return jnp.cos(ang), jnp.sin(ang)


def apply_rope(x, cos, sin):
    xf = x.astype(jnp.float32)
    x1, x2 = jnp.split(xf, 2, axis=-1)
    c = cos[None, :, None, :]
    s = sin[None, :, None, :]
    return jnp.concatenate([x1 * c - x2 * s, x2 * c + x1 * s], axis=-1).astype(x.dtype)


def causal_depthwise_conv(u, w):
    K = w.shape[0]
    return lax.conv_general_dilated(
        u, w[:, None, :].astype(u.dtype), window_strides=(1,), padding=[(K - 1, 0)],
        dimension_numbers=('NWC', 'WIO', 'NWC'), feature_group_count=u.shape[-1])


def moba_attention(q, k, v):
    B, T, H, dh = q.shape
    L = MOBA_BLOCK
    nb = -(-T // L)
    tp = nb * L
    pad = ((0, 0), (0, 0), (0, tp - T), (0, 0))
    qh = jnp.pad(q.transpose(0, 2, 1, 3), pad)
    kh = jnp.pad(k.transpose(0, 2, 1, 3), pad)
    vh = jnp.pad(v.transpose(0, 2, 1, 3), pad)
    kb = kh.reshape(B, H, nb, L, dh)
    vb = vh.reshape(B, H, nb, L, dh)
    kbar = jnp.mean(kb.astype(jnp.float32), axis=3)
    qblk = jnp.arange(tp) // L
    gate = jnp.einsum('bhtd,bhnd->bhtn', qh.astype(jnp.float32), kbar)
    past = jnp.arange(nb)[None, :] < qblk[:, None]
    gate = jnp.where(past[None, None], gate, -jnp.inf)
    k_sel = min(MOBA_TOPK, nb)
    _, idx = lax.top_k(gate, k_sel)
    valid = idx < qblk[None, None, :, None]
    nc = tp // Q_CHUNK
    q_c = qh.reshape(B, H, nc, Q_CHUNK, dh).transpose(2, 0, 1, 3, 4)
    idx_c = idx.reshape(B, H, nc, Q_CHUNK, k_sel).transpose(2, 0, 1, 3, 4)
    val_c = valid.reshape(B, H, nc, Q_CHUNK, k_sel).transpose(2, 0, 1, 3, 4)
    starts = jnp.arange(nc, dtype=jnp.int32) * Q_CHUNK
    bi = jnp.arange(B)[:, None, None, None]
    hi = jnp.arange(H)[None, :, None, None]
    scale = dh ** -0.5

    def chunk(args):
        qc, ic, vc, start = args
        ks = kb[bi, hi, ic]
        vs = vb[bi, hi, ic]
        s_past = jnp.einsum('bhqd,bhqnld->bhqnl', qc, ks).astype(jnp.float32) * scale
        s_past = jnp.where(vc[..., None], s_past, -jnp.inf).reshape(B, H, Q_CHUNK, k_sel * L)
        blk = start // L
        ko = lax.dynamic_index_in_dim(kb, blk, axis=2, keepdims=False)
        vo = lax.dynamic_index_in_dim(vb, blk, axis=2, keepdims=False)
        s_own = jnp.einsum('bhqd,bhld->bhql', qc, ko).astype(jnp.float32) * scale
        qpos = start + jnp.arange(Q_CHUNK)
        kpos = blk * L + jnp.arange(L)
        s_own = jnp.where((kpos[None, :] <= qpos[:, None])[None, None], s_own, -jnp.inf)
        p = jax.nn.softmax(jnp.concatenate([s_past, s_own], axis=-1), axis=-1)
        p_past = p[..., :k_sel * L].reshape(B, H, Q_CHUNK, k_sel, L).astype(vs.dtype)
        p_own = p[..., k_sel * L:].astype(vo.dtype)
        return (jnp.einsum('bhqnl,bhqnld->bhqd', p_past, vs)
                + jnp.einsum('bhql,bhld->bhqd', p_own, vo))

    out = lax.map(chunk, (q_c, idx_c, val_c, starts))
    out = out.transpose(1, 0, 3, 2, 4).reshape(B, tp, H, dh)
    return out[:, :T]


def multiscale_pool(u, w_grp, scale):
    B, T, _ = u.shape
    ug = u.reshape(B, T, POOL_GROUPS, POOL_GW).astype(jnp.float32)
    cs = jnp.concatenate([jnp.zeros((B, 1, POOL_GROUPS, POOL_GW), jnp.float32),
                          jnp.cumsum(ug, axis=1)], axis=1)
    win = jnp.array(POOL_WINDOWS, dtype=jnp.int32)
    hi = jnp.arange(T, dtype=jnp.int32) + 1
    lo = jnp.maximum(hi[:, None] - win[None, :], 0)
    gi = jnp.arange(POOL_GROUPS)[None, :]
    window_sum = cs[:, 1:] - cs[:, lo, gi]
    count = (hi[:, None] - lo).astype(jnp.float32)
    d = window_sum / count[None, :, :, None] - ug
    y = jnp.einsum('btgc,gcd->btgd', d, w_grp.astype(jnp.float32)).reshape(B, T, POOL_WIDTH)
    return (y * scale.astype(jnp.float32)).astype(u.dtype)


def setup_inputs(seed: int = 0) -> dict:
    key = jax.random.key(seed)
    ks = jax.random.split(key, 22)
    f32 = jnp.float32

    def nrm(k, shape, s):
        return jax.random.normal(k, shape, f32) * s

    L, D = DEPTH, D_MODEL
    return {
        'x': nrm(ks[0], (BATCH, SEQ, D), 1.0),
        'w_in': nrm(ks[1], (L, D, IN_WIDTH), D ** -0.5),
        'b_in': nrm(ks[2], (L, IN_WIDTH), 0.02),
        'w_dw_conf': nrm(ks[3], (L, CONF_KERNEL, CONF_WIDTH), CONF_KERNEL ** -0.5),
        'b_dw_conf': nrm(ks[4], (L, CONF_WIDTH), 0.02),
        'ln_conf_g': 1.0 + nrm(ks[5], (L, CONF_WIDTH), 0.02),
        'ln_conf_b': nrm(ks[6], (L, CONF_WIDTH), 0.02),
        'w_pool': nrm(ks[7], (L, POOL_GROUPS, POOL_GW, POOL_GW), POOL_GW ** -0.5),
        'pool_scale': 1.0 + nrm(ks[8], (L, POOL_WIDTH), 0.1),
        'w_sc': nrm(ks[9], (L, SC_KERNEL, SC_WIDTH), SC_KERNEL ** -0.5),
        'w_pa': nrm(ks[10], (L, ATT_WIDTH, D), ATT_WIDTH ** -0.5),
        'w_pb': nrm(ks[11], (L, CONF_WIDTH, D), CONF_WIDTH ** -0.5),
        'w_pc': nrm(ks[12], (L, POOL_WIDTH, D), POOL_WIDTH ** -0.5),
        'w_pd': nrm(ks[13], (L, SC_WIDTH, D), SC_WIDTH ** -0.5),
        'w_o': nrm(ks[14], (L, D, D), D ** -0.5 * DEEPNORM_BETA),
        'ln1_g': 1.0 + nrm(ks[15], (L, D), 0.02),
        'ln1_b': nrm(ks[16], (L, D), 0.02),
        'w_mlp1': nrm(ks[17], (L, D, D_FF), D ** -0.5),
        'w_mlp2': nrm(ks[18], (L, D_FF, D), D_FF ** -0.5 * DEEPNORM_BETA),
        'ln2_g': 1.0 + nrm(ks[19], (L, D), 0.02),
        'ln2_b': nrm(ks[20], (L, D), 0.02),
    }


def reference(x, w_in, b_in, w_dw_conf, b_dw_conf, ln_conf_g, ln_conf_b, w_pool,
              pool_scale, w_sc, w_pa, w_pb, w_pc, w_pd, w_o, ln1_g, ln1_b,
              w_mlp1, w_mlp2, ln2_g, ln2_b):
    B, T, _ = x.shape
    cos, sin = rope_tables(T)
    for l in range(DEPTH):
        z = x @ w_in[l] + b_in[l]
        q, k, v, glu, pool_in, sc_x, sc_b, sc_c, gate_logits = jnp.split(z, SPLIT_POINTS, axis=-1)
        q = apply_rope(q.reshape(B, T, ATT_HEADS, HEAD_DIM), cos, sin)
        k = apply_rope(k.reshape(B, T, ATT_HEADS, HEAD_DIM), cos, sin)
        v = v.reshape(B, T, ATT_HEADS, HEAD_DIM)
        y_a = moba_attention(q, k, v).reshape(B, T, ATT_WIDTH)
        a, g = jnp.split(glu, 2, axis=-1)
        u = causal_depthwise_conv(a * jax.nn.sigmoid(g), w_dw_conf[l]) + b_dw_conf[l]
        y_b = jax.nn.silu(layer_norm(u, ln_conf_g[l], ln_conf_b[l]))
        y_c = multiscale_pool(pool_in, w_pool[l], pool_scale[l])
        y_d = sc_b * causal_depthwise_conv(sc_c * sc_x, w_sc[l])
        gates = jax.nn.sigmoid(gate_logits.reshape(B, T, N_BRANCH, D_MODEL))
        mix = (gates[:, :, 0] * (y_a @ w_pa[l]) + gates[:, :, 1] * (y_b @ w_pb[l])
               + gates[:, :, 2] * (y_c @ w_pc[l]) + gates[:, :, 3] * (y_d @ w_pd[l]))
        x = layer_norm(DEEPNORM_ALPHA * x + mix @ w_o[l], ln1_g[l], ln1_b[l])
        h = jnp.square(jax.nn.relu(x @ w_mlp1[l]))
        x = layer_norm(DEEPNORM_ALPHA * x + h @ w_mlp2[l], ln2_g[l], ln2_b[l])
    return x
```

```python
import numpy as np
from contextlib import ExitStack
import concourse.bass as bass
import concourse.mybir as mybir
from concourse.bass_utils import run_bass_kernel_spmd

F32 = mybir.dt.float32
BF16 = mybir.dt.bfloat16
AF = mybir.ActivationFunctionType
ALU = mybir.AluOpType
AX = mybir.AxisListType

D_MODEL = 1024
SEQ = 2048
DEPTH = 4
NCORES = 8
ALPHA = (2.0 * DEPTH) ** 0.25
INV_ALPHA = 1.0 / ALPHA
LN_EPS = 1e-5
NEG = -30000.0

PAGE = 256
ENGS = ['pe', 'act', 'dve', 'pool', 'sp']
NDSEM = 8
MAXEPOCH = 8

NSLOT = 34
NSM = 9
NPAR = 176
NCF = 4136
NCB = 4064


class Op:
    __slots__ = ('eng', 'fn', 'deps', 'inc', 'epoch', 'dma', 'tok', 'idx', 'prewait')

    def __init__(self, eng, fn, epoch, dma):
        self.eng = eng
        self.fn = fn
        self.deps = []
        self.inc = False
        self.epoch = epoch
        self.dma = dma
        self.tok = None
        self.prewait = None


class Sched:
    def __init__(self, nc):
        self.nc = nc
        self.ops = {e: [] for e in ENGS}
        self.last_w = {}
        self.readers = {}
        self.epoch = 0
        self.tbase = {}
        self.dma_uses = {}
        self.dma_rr = {e: 0 for e in ENGS}
        self._pcache = {}

    def sb(self, name, shape, dtype, offset):
        t = self.nc.alloc_sbuf_tensor_at(name, list(shape), dtype, offset=offset)
        self.tbase[t.name] = ('sb', offset)
        return t

    def reg_psum(self, t, bank):
        self.tbase[t.name] = ('ps', bank * 2048)

    def keys(self, a):
        t = a.tensor
        ck = (t.name, a.offset, a.ap, a.dtype)
        r = self._pcache.get(ck)
        if r is not None:
            return r
        space, base = self.tbase[t.name]
        isz = mybir.dt.size(a.dtype)
        ap = a.ap
        pstep, pcnt = ap[0]
        if pstep == 0:
            p0 = 0
            foff = a.offset
        else:
            p0 = a.offset // pstep
            foff = a.offset % pstep
        q0, q1 = p0 // 32, (p0 + pcnt - 1) // 32
        dims = [d for d in ap[1:] if d[1] > 1 and d[0] != 0]
        if not dims:
            dims = [(1, 1)]
        inner = dims[-1]
        outer = dims[:-1]
        runlen = (inner[1] - 1) * abs(inner[0]) + 1
        pages = set()
        idx = [0] * len(outer)
        while True:
            st = foff + sum(i * d[0] for i, d in zip(idx, outer))
            b0 = base + st * isz
            b1 = base + (st + runlen) * isz - 1
            for pg in range(b0 // PAGE, b1 // PAGE + 1):
                pages.add(pg)
            k = len(outer) - 1
            while k >= 0:
                idx[k] += 1
                if idx[k] < outer[k][1]:
                    break
                idx[k] = 0
                k -= 1
            if k < 0:
                break
        if space == 'ps':
            banks = set(pg * PAGE // 2048 for pg in pages)
            r = [(space, b, q) for b in banks for q in range(q0, q1 + 1)]
        else:
            r = [(space, pg, q) for pg in pages for q in range(q0, q1 + 1)]
        self._pcache[ck] = r
        return r

    def op(self, eng, fn, reads=(), writes=(), dma=False):
        o = Op(eng, fn, self.epoch, dma)
        deps = set()
        ps_reads = [a for a in reads if self.tbase[a.tensor.name][0] == 'ps']
        if ps_reads:
            reads = [a for a in reads if self.tbase[a.tensor.name][0] != 'ps']
            writes = list(writes) + ps_reads
        for a in reads:
            for k in self.keys(a):
                w = self.last_w.get(k)
                if w is not None:
                    deps.add(w)
                self.readers.setdefault(k, []).append(o)
        for a in writes:
            for k in self.keys(a):
                w = self.last_w.get(k)
                if w is not None:
                    deps.add(w)
                rs = self.readers.get(k)
                if rs:
                    deps.update(rs)
                self.last_w[k] = o
                self.readers[k] = []
        deps.discard(o)
        o.deps = list(deps)
        for d in o.deps:
            if not d.dma and not (d.eng == 'pe' and eng == 'pe'):
                d.inc = True
        o.idx = len(self.ops[eng])
        self.ops[eng].append(o)
        if dma:
            k = self.dma_rr[eng] % NDSEM
            self.dma_rr[eng] += 1
            u = self.dma_uses.get((eng, k), 0)
            o.prewait = ((eng, k), 16 * u)
            self.dma_uses[(eng, k)] = u + 1
            o.tok = (('d', eng, k), 16 * (u + 1))
        return o

    def new_epoch(self):
        self.epoch += 1
        assert self.epoch < MAXEPOCH

    def emit(self, final_waits=()):
        nc = self.nc
        for e in ENGS:
            cnt = {}
            for o in self.ops[e]:
                if o.dma:
                    continue
                if o.inc:
                    c = cnt.get(o.epoch, 0) + 1
                    cnt[o.epoch] = c
                    o.tok = (('e', e, o.epoch), c)
        with ExitStack() as es:
            sems = {}
            for e in ENGS:
                for ep in range(self.epoch + 1):
                    sems[('e', e, ep)] = es.enter_context(nc.semaphore(f"s_{e}_{ep}"))
            for (e, k) in self.dma_uses:
                sems[('d', e, k)] = es.enter_context(nc.semaphore(f"d_{e}_{k}"))
            block = es.enter_context(nc.Block())
            handles = {'pe': block.tensor, 'act': block.scalar, 'dve': block.vector,
                       'pool': block.gpsimd, 'sp': block.sync}

            def make(e):
                def body(h):
                    seen = {}
                    for o in self.ops[e]:
                        toks = []
                        for d in o.deps:
                            if d.eng == e and e == 'pe' and not d.dma:
                                continue
                            toks.append(d.tok)
                        if o.dma:
                            (qe, k), v = o.prewait
                            if v > 0:
                                toks.append((('d', qe, k), v))
                        best = {}
                        for (s, v) in toks:
                            if v > best.get(s, 0):
                                best[s] = v
                        for s, v in best.items():
                            if seen.get(s, 0) >= v:
                                continue
                            h.wait_ge(sems[s], v)
                            seen[s] = v
                        ins = o.fn(h)
                        if o.dma:
                            ins.then_inc(sems[o.tok[0]], 16)
                        elif o.inc:
                            ins.then_inc(sems[o.tok[0]], 1)
                    if e == 'sp':
                        for o in final_waits:
                            s, v = o.tok
                            if seen.get(s, 0) < v:
                                h.wait_ge(sems[s], v)
                                seen[s] = v
                return body
            for e in ENGS:
                handles[e](make(e))


def build_program(n_layers=DEPTH, debug=False, stop=None):
    nc = bass.Bass("TRN2", target_bir_lowering=False)
    S = Sched(nc)
    L = n_layers
    T = SEQ

    xT_d = nc.dram_tensor("xT", [1024, T], F32, kind="ExternalInput").ap()
    wbig_d = nc.dram_tensor("wbig", [L, NSLOT, 128, 4096], F32, kind="ExternalInput").ap()
    wsm_d = nc.dram_tensor("wsm", [L, NSM, 128, 1280], F32, kind="ExternalInput").ap()
    par_d = nc.dram_tensor("par", [128, L * NPAR], F32, kind="ExternalInput").ap()
    cstf_d = nc.dram_tensor("cstf", [128, NCF], F32, kind="ExternalInput").ap()
    cstb_d = nc.dram_tensor("cstb", [128, NCB], F32, kind="ExternalInput").ap()
    outT_d = nc.dram_tensor("outT", [1024, T], F32, kind="ExternalOutput").ap()
    dbg_d = None
    if debug:
        dbg_d = nc.dram_tensor("dbg", [8, 128, 8 * T], F32, kind="ExternalOutput").ap()

    BASE = 16640
    LIMIT = 229344
    cur = [BASE]
    names = [0]

    def alloc(shape, dt, at=None):
        n = int(np.prod(shape[1:])) * mybir.dt.size(dt)
        n = (n + 63) // 64 * 64
        if at is None:
            off = (cur[0] + PAGE - 1) // PAGE * PAGE
            cur[0] = off + n
        else:
            off = at
        assert off + n <= LIMIT, (off, n)
        names[0] += 1
        return S.sb(f"t{names[0]}", shape, dt, off)

    xhi = alloc([128, 8, T], BF16)
    xlo = alloc([128, 8, T], BF16)
    par = alloc([128, L * NPAR], F32)
    csm = alloc([128, 40], F32)
    cb = alloc([128, NCB], BF16)
    wslot = [alloc([128, 4096], BF16) for _ in range(3)]
    wsml = [alloc([128, 1280], BF16) for _ in range(2)]
    D0 = (cur[0] + PAGE - 1) // PAGE * PAGE
    DYN = LIMIT - D0
    assert DYN >= 104900, DYN

    def at(off, shape, dt):
        return alloc(shape, dt, at=D0 + off)

    ident = cb[:, 0:128]
    onesD = cb[:, 128:256]
    ones256 = cb[:, 256:384]
    ones1 = cb[:, 384:448]
    cmA = cb[:, 448:704]
    cmB = cb[:, 704:960]

    def oh(n, hh=0):
        return cb[hh * 64: hh * 64 + 8, 960 + n * 128: 960 + (n + 1) * 128]

    def cm4(i):
        return cb[:, 2016 + i * 512: 2016 + (i + 1) * 512]

    def ohf(n):
        return cb[:, 960 + n * 128: 960 + (n + 1) * 128]

    def selc(q):
        return cb[0:64, 1984 + q * 8: 1984 + (q + 1) * 8]
    invw = csm[:, 0:2]
    f16 = csm[:, 2:34]
    eps_main = csm[:, 34:35]
    eps_conf = csm[:, 35:36]

    ya = at(0, [128, 4, T], BF16)
    yb = at(16384, [128, 2, T], BF16)
    yc = at(24576, [128, 2, T], BF16)
    yd = at(32768, [128, 2, T], BF16)
    cos_t = at(40960, [128, T], F32)
    sin_t = at(49152, [128, T], F32)
    ATT = 57344
    qbuf = [at(ATT, [128, T], BF16)]
    kpad = [at(ATT + 4096 + i * 4096, [128, T], BF16) for i in range(2)]
    vall = at(ATT + 16384, [128, 16, 512], BF16)
    A2 = ATT + 32768
    pT = [at(A2 + i * 1024, [128, 512], BF16) for i in range(4)]
    rcb = [at(A2 + 4096, [128, 512], F32)]
    ntmp = [at(A2 + 6144, [128, 512], F32)]
    rt1 = [at(ATT + 12288, [128, 512], F32)] * 2
    rt2 = [at(ATT + 14336, [128, 512], F32)] * 2
    biasT = [at(A2 + 8192 + i * 1024, [128, 512], BF16) for i in range(4)]
    itile = [at(A2 + 12288 + i * 512, [64, 256], BF16) for i in range(4)]
    kbar = [at(A2 + 14336 + i * 64, [128, 8], F32) for i in range(2)]
    kdiff = [at(A2 + 14464 + i * 128, [128, 64], BF16) for i in range(2)]
    assert A2 + 14720 <= 104900
    SA = 40960
    cub = at(65536, [128, 2, 30 + T], BF16)
    crb = at(73856, [128, 2, 512], BF16)
    csq = at(75904, [128, 2, 512], BF16)
    csg = [at(82048 + i * 2048, [128, 512], F32) for i in range(2)]
    dg = at(86144, [128, 2, 31, 128], BF16)
    cst = [at(77952, [128, 512], F32), at(80000, [128, 512], F32), at(102016, [128, 512], F32)]
    pup = at(SA, [128, 32 + T], F32)
    psA = at(SA + 8320, [128, 32 + T], F32)
    psB = at(SA + 16640, [128, 32 + T], F32)
    pd = at(SA + 24960, [128, T], BF16)
    pt16 = at(SA + 29056, [128, 16], F32)
    scx = at(SA + 29184, [128, 2, T], F32)
    scp = at(SA + 45568, [128, 2, 2 + T], F32)
    assert SA + 62016 <= 104900
    mix = at(40960, [128, 8, T], BF16)
    msg = [at(73728 + i * 2048, [128, 512], F32) for i in range(3)]
    mtm = [at(79872 + i * 2048, [128, 512], F32) for i in range(2)]
    macc = [at(83968 + i * 2048, [128, 512], F32) for i in range(2)]
    r32 = [at(i * 16384, [128, 8, 512], F32) for i in range(2)]
    lrb = at(73728, [128, 8, 512], BF16)
    lsq = at(81920, [128, 8, 512], BF16)
    lst = [[at(98304 + i * 2048, [128, 512], F32) for i in range(3)],
           [at(90112 + i * 2048, [128, 512], F32) for i in range(3)]]
    acc = at(0, [128, 8, T], F32)
    hbuf = at(65536, [128, 8, T], BF16)
    htm = [at(98304 + i * 2048, [128, 512], F32) for i in range(3)]
    lrb2 = at(65536, [128, 8, 512], BF16)
    lsq2 = at(73728, [128, 8, 512], BF16)

    ps = []
    for i in range(8):
        t = nc.alloc_psum_tensor(f"ps{i}", [128, 512], F32)
        S.reg_psum(t, i)
        ps.append(t)
    g_rr = [0]
    l_rr = [0]

    def gbank():
        b = ps[g_rr[0] % 4]
        g_rr[0] += 1
        return b

    def lbank():
        b = ps[4 + l_rr[0] % 4]
        l_rr[0] += 1
        return b

    def isap(v):
        return not isinstance(v, (int, float)) and v is not None

    def mm(out, lhsT, rhs, start, stop, **kw):
        S.op('pe', lambda h: h.matmul(out, lhsT=lhsT, rhs=rhs, start=start, stop=stop, **kw),
             reads=[lhsT, rhs], writes=[out])

    def act(out, in_, func, bias=None, scale=None, eng='act'):
        rd = [in_]
        kw = {}
        if bias is not None:
            kw['bias'] = bias
            if isap(bias):
                rd.append(bias)
        if scale is not None:
            kw['scale'] = scale
            if isap(scale):
                rd.append(scale)
        S.op('act', lambda h: h.activation(out=out, in_=in_, func=func, **kw), reads=rd, writes=[out])

    def tt(out, in0, in1, op, eng='dve'):
        S.op(eng, lambda h: h.tensor_tensor(out=out, in0=in0, in1=in1, op=op), reads=[in0, in1], writes=[out])

    def ts(out, in0, s1, s2, op0, op1=None, eng='dve'):
        rd = [in0] + [s for s in (s1, s2) if isap(s)]
        if op1 is None:
            S.op(eng, lambda h: h.tensor_scalar(out=out, in0=in0, scalar1=s1, scalar2=None, op0=op0),
                 reads=rd, writes=[out])
        else:
            S.op(eng, lambda h: h.tensor_scalar(out=out, in0=in0, scalar1=s1, scalar2=s2, op0=op0, op1=op1),
                 reads=rd, writes=[out])

    def stt(out, in0, scalar, in1, op0, op1, eng='dve'):
        rd = [in0, in1] + ([scalar] if isap(scalar) else [])
        S.op(eng, lambda h: h.scalar_tensor_tensor(out=out, in0=in0, scalar=scalar, in1=in1, op0=op0, op1=op1),
             reads=rd, writes=[out])

    def cpy(out, in_, eng='dve'):
        S.op(eng, lambda h: h.tensor_copy(out=out, in_=in_), reads=[in_], writes=[out])

    def dma(q, out, in_, reads=(), writes=()):
        return S.op(q, lambda h: h.dma_start(out=out, in_=in_), reads=reads, writes=writes, dma=True)

    def memset(ap, val, eng='dve'):
        S.op(eng, lambda h: h.memset(ap, val), writes=[ap])

    wq = []
    for l in range(L):
        for s in range(NSLOT):
            wq.append((l, s))
    wstate = {'issued': 0, 'cons': 0}

    def w_issue():
        i = wstate['issued']
        if i >= len(wq):
            return
        l, s = wq[i]
        buf = wslot[i % 3]
        dma('pool', buf[:], wbig_d[l, s], writes=[buf[:]])
        wstate['issued'] += 1

    def w_next(l, s, keep=0):
        i = wstate['cons']
        assert wq[i] == (l, s), (wq[i], l, s)
        wstate['cons'] += 1
        while wstate['issued'] < min(i - keep + 3, len(wq)):
            w_issue()
        return wslot[i % 3]

    smq = [(l, s) for l in range(L) for s in range(NSM)]
    smstate = {'issued': 0, 'cons': 0}

    def sm_issue():
        i = smstate['issued']
        if i >= len(smq):
            return
        l, s = smq[i]
        buf = wsml[i % 2]
        dma('pool', buf[:], wsm_d[l, s], writes=[buf[:]])
        smstate['issued'] += 1

    def sm_next(l, s):
        i = smstate['cons']
        assert smq[i] == (l, s)
        smstate['cons'] += 1
        while smstate['issued'] < min(i + 2, len(smq)):
            sm_issue()
        return wsml[i % 2]

    dma('sp', par[:], par_d, writes=[par[:]])
    dma('sp', csm[:, 0:36], cstf_d[:, 4096:4132], writes=[csm[:, 0:36]])
    dma('pool', cb[:], cstb_d, writes=[cb[:]])
    xin = acc
    dma('sp', xin[:], xT_d.rearrange("(c p) t -> p c t", p=128), writes=[xin[:]])
    for c in range(8):
        act(xhi[:, c, :], xin[:, c, :], AF.Copy)
        tt(xlo[:, c, :], xin[:, c, :], xhi[:, c, :], ALU.subtract)
    w_issue()
    w_issue()
    sm_issue()

    def TC(tc):
        return slice(tc * 512, (tc + 1) * 512)

    out_dmas = []

    def ln_stats(pset, rb, sq):
        b1 = lbank()
        b2 = lbank()
        for c in range(8):
            mm(b1[:], onesD, rb[:, c, :], c == 0, c == 7)
        for c in range(8):
            mm(b2[:], onesD, sq[:, c, :], c == 0, c == 7)
        m2, sd, nmr = lst[pset][0][:], lst[pset][1][:], lst[pset][2][:]
        act(m2, b1[:], AF.Square)
        tt(m2, b2[:], m2, ALU.subtract)
        act(sd, m2, AF.Sqrt, bias=eps_main)
        S.op('dve', lambda h: h.reciprocal(out=sd, in_=sd), reads=[sd], writes=[sd])
        stt(nmr, b1[:], -1.0, sd, ALU.mult, ALU.mult)
        return sd, nmr

    def ln_apply_chunk(l, tc, rin, sd, nmr, gcol, bcol, last, c):
        r = rin[c]
        tt(r, r, sd, ALU.mult)
        tt(r, r, nmr, ALU.add)
        g = par[:, l * NPAR + gcol + c: l * NPAR + gcol + c + 1]
        b = par[:, l * NPAR + bcol + c: l * NPAR + bcol + c + 1]
        act(r, r, AF.Identity, bias=b, scale=g)
        if not last:
            cpy(xhi[:, c, TC(tc)], r, eng='pool')
            tt(xlo[:, c, TC(tc)], r, xhi[:, c, TC(tc)], ALU.subtract, eng='pool')

    def ln_apply(l, tc, rin, sd, nmr, gcol, bcol, last):
        for c in range(8):
            r = rin[c]
            tt(r, r, sd, ALU.mult)
            tt(r, r, nmr, ALU.add)
        for c in range(8):
            r = rin[c]
            g = par[:, l * NPAR + gcol + c: l * NPAR + gcol + c + 1]
            b = par[:, l * NPAR + bcol + c: l * NPAR + bcol + c + 1]
            act(r, r, AF.Identity, bias=b, scale=g)
        if not last:
            for c in range(8):
                cpy(xhi[:, c, TC(tc)], rin[c], eng='pool')
            for c in range(8):
                tt(xlo[:, c, TC(tc)], rin[c], xhi[:, c, TC(tc)], ALU.subtract, eng='pool')

    def finish(dumps):
        outs = []
        for i, a in enumerate(dumps):
            n = a.shape[1]
            outs.append(dma('pool', dbg_d[i, 0:a.shape[0], 0:n], a, reads=[a]))
        S.emit(final_waits=outs)
        return nc

    def fl(t):
        a = t[:]
        if len(a.shape) == 3:
            a = a.rearrange("p a b -> p (a b)")
        return a

    for l in range(L):
        if l > 0:
            S.new_epoch()
        P = l * NPAR

        def bcol(slot, j):
            return par[:, P + slot * 4 + j: P + slot * 4 + j + 1]

        def pcol(i):
            return par[:, P + i: P + i + 1]

        def inproj(bank, ws, j, tc):
            wv = ws[:].rearrange("p (k n) -> p k n", k=8)
            for kc in range(8):
                mm(bank[:], wv[:, kc, j * 128:(j + 1) * 128], xhi[:, kc, TC(tc)], kc == 0, kc == 7)

        ws = w_next(l, 0)
        for cc in range(2):
            memset(cub[:, cc, 0:30], 0.0)
        for tc in range(4):
            for cc in range(2):
                ba = gbank()
                bg = gbank()
                inproj(ba, ws, cc, tc)
                inproj(bg, ws, 2 + cc, tc)
                sg = csg[cc][:]
                act(sg, bg[:], AF.Sigmoid, bias=bcol(0, 2 + cc))
                stt(cub[:, cc, 30 + tc * 512: 30 + (tc + 1) * 512], ba[:], bcol(0, cc), sg, ALU.add, ALU.mult)
        for cc in range(2):
            dgv = dg[:, cc, :, :]
            idb = bass.AP(ident.tensor, ident.offset, [list(ident.ap[0]), [0, 31], [1, 128]])
            wc_ = par[:, P + 64 + cc * 31: P + 64 + (cc + 1) * 31]
            wb = bass.AP(wc_.tensor, wc_.offset, [list(wc_.ap[0]), [1, 31], [0, 128]])
            S.op('dve', lambda h, dgv=dgv, idb=idb, wb=wb: h.tensor_tensor(out=dgv, in0=idb, in1=wb, op=ALU.mult),
                 reads=[ident, wc_], writes=[dgv])
        def conv_mm(tc):
            bks_ = []
            for cc in range(2):
                bk = gbank()
                bks_.append(bk)
                for k in range(31):
                    mm(bk[:], dg[:, cc, k, :], cub[:, cc, tc * 512 + k: tc * 512 + k + 512], k == 0, k == 30)
            return bks_
        cnext = conv_mm(0)
        for tc in range(4):
            cbk = cnext
            if tc < 3:
                cnext = conv_mm(tc + 1)
            for cc in range(2):
                act(crb[:, cc, :], cbk[cc][:], AF.Identity, bias=pcol(126 + cc))
                act(csq[:, cc, :], cbk[cc][:], AF.Square, bias=pcol(126 + cc))
            b1 = lbank()
            b2 = lbank()
            for cc in range(2):
                mm(b1[:], ones256, crb[:, cc, :], cc == 0, cc == 1)
            for cc in range(2):
                mm(b2[:], ones256, csq[:, cc, :], cc == 0, cc == 1)
            m2, sd, nmr = cst[0][:], cst[1][:], cst[2][:]
            act(m2, b1[:], AF.Square)
            tt(m2, b2[:], m2, ALU.subtract)
            act(sd, m2, AF.Sqrt, bias=eps_conf)
            S.op('dve', lambda h, sd=sd: h.reciprocal(out=sd, in_=sd), reads=[sd], writes=[sd])
            stt(nmr, b1[:], -1.0, sd, ALU.mult, ALU.mult)
            for cc in range(2):
                tmp = csg[1][:] if cc == 0 else csg[0][:]
                stt(tmp, cbk[cc][:], pcol(126 + cc), sd, ALU.add, ALU.mult)
                tt(tmp, tmp, nmr, ALU.add)
                act(yb[:, cc, TC(tc)], tmp, AF.Silu, bias=pcol(130 + cc), scale=pcol(128 + cc))

        ws6 = w_next(l, 1)
        wpool = sm_next(l, 0)
        wpv = wpool[:, 0:256].rearrange("p (c n) -> p c n", c=2)
        for cc in range(2):
            memset(pup[:, 0:32], 0.0)
            for tc in range(4):
                b = gbank()
                inproj(b, ws6, cc, tc)
                act(pup[:, 32 + tc * 512: 32 + (tc + 1) * 512], b[:], AF.Identity, bias=bcol(1, cc))
            E = 32 + T
            tt(psA[:, 1:E], pup[:, 1:E], pup[:, 0:E - 1], ALU.add)
            tt(psB[:, 3:E], psA[:, 3:E], psA[:, 1:E - 2], ALU.add)
            if cc == 1:
                tt(psA[:, 7:E], psB[:, 7:E], psB[:, 3:E - 4], ALU.add)
                tt(psB[:, 15:E], psA[:, 15:E], psA[:, 7:E - 8], ALU.add)
            cpy(psB[0:64, 32:E], psA[0:64, 32:E])
            stt(pd[:, :], psB[:, 32:E], invw[:, cc:cc + 1], pup[:, 32:E], ALU.mult, ALU.subtract)
            tt(pt16[:, :], psB[:, 32:48], f16[:, cc * 16:(cc + 1) * 16], ALU.mult)
            tt(pd[:, 0:16], pt16[:, :], pup[:, 32:48], ALU.subtract)
            for tc in range(4):
                b = gbank()
                mm(b[:], wpv[:, cc, :], pd[:, TC(tc)], True, True)
                act(yc[:, cc, TC(tc)], b[:], AF.Identity, scale=pcol(132 + cc))
        for cc in range(2):
            for tc in range(4):
                b = gbank()
                inproj(b, ws6, 2 + cc, tc)
                act(scx[:, cc, TC(tc)], b[:], AF.Identity, bias=bcol(1, 2 + cc))
        ws7 = w_next(l, 2)
        for cc in range(2):
            memset(scp[:, cc, 0:2], 0.0)
            for tc in range(4):
                b = gbank()
                inproj(b, ws7, cc, tc)
                stt(scp[:, cc, 2 + tc * 512: 2 + (tc + 1) * 512], b[:], bcol(2, cc), scx[:, cc, TC(tc)],
                    ALU.add, ALU.mult)
            cv = scx[:, cc, :]
            ts(cv, scp[:, cc, 0:T], pcol(134 + cc * 3 + 0), None, ALU.mult)
            stt(cv, scp[:, cc, 1:T + 1], pcol(134 + cc * 3 + 1), cv, ALU.mult, ALU.add)
            stt(cv, scp[:, cc, 2:T + 2], pcol(134 + cc * 3 + 2), cv, ALU.mult, ALU.add)
            for tc in range(4):
                b = gbank()
                inproj(b, ws7, 2 + cc, tc)
                stt(yd[:, cc, TC(tc)], b[:], bcol(2, 2 + cc), scx[:, cc, TC(tc)], ALU.add, ALU.mult)

        if stop == 'A':
            return finish([fl(yb), fl(yc), fl(yd)])
        dma('sp', cos_t[:], cstf_d[:, 0:T], writes=[cos_t[:]])
        dma('sp', sin_t[:], cstf_d[:, T:2 * T], writes=[sin_t[:]])
        wsv = w_next(l, 3)
        wvv = wsv[:].rearrange("p (k n) -> p k n", k=8)
        for tt_ in range(16):
            b = gbank()
            for kc in range(8):
                mm(b[:], xhi[:, kc, tt_ * 128:(tt_ + 1) * 128], wvv[:, kc, :], kc == 0, kc == 7)
            if tt_ % 2 == 0:
                act(vall[:, tt_, :], b[:], AF.Copy)
            else:
                cpy(vall[:, tt_, :], b[:])
        if stop == 'B0':
            return finish([fl(vall)])
        nb_rr = 0
        pt_rr = [0]
        memset(kpad[0][64:128, :], 0.0)
        memset(kpad[1][0:64, :], 0.0)
        for i_ in range(4):
            memset(biasT[i_][:, :], 0.0)
        for c in range(4):
            if stop is not None and stop.startswith('B3') and c == int(stop[2:]):
                return finish([fl(ya), fl(qbuf[0]), fl(kpad[0]), fl(vall)])
            wsp = w_next(l, 4 + c)
            qb_ = qbuf[0]
            kb8 = kbar[c % 2]
            kd = kdiff[c % 2]

            def rows_(hh):
                return slice(hh * 64, (hh + 1) * 64)

            def QS_(qb):
                return slice(qb * 256, (qb + 1) * 256)

            def emit_rope(tcs):
                for tc in tcs:
                    for which in (0, 1):
                        bz = gbank()
                        bs = gbank()
                        inproj(bz, wsp, which * 2, tc)
                        inproj(bs, wsp, which * 2 + 1, tc)
                        t1 = rt1[tc % 2][:]
                        t2 = rt2[tc % 2][:]
                        stt(t1, bz[:], bcol(4 + c, which * 2), cos_t[:, TC(tc)], ALU.add, ALU.mult)
                        stt(t2, bs[:], bcol(4 + c, which * 2 + 1), sin_t[:, TC(tc)], ALU.add, ALU.mult)
                        if which == 0:
                            tt(qb_[:, TC(tc)], t1, t2, ALU.add)
                        else:
                            tt(kpad[0][0:64, TC(tc)], t1[0:64, :], t2[0:64, :], ALU.add)
                            tt(kpad[1][64:128, TC(tc)], t1[64:128, :], t2[64:128, :], ALU.add)

            def emit_kdiff():
                for hh_ in range(2):
                    pr_ = slice(hh_ * 64, (hh_ + 1) * 64)
                    src_ = kpad[hh_][pr_, :]
                    dst_ = kb8[pr_, :]
                    S.op('dve', lambda h, src_=src_, dst_=dst_: h.tensor_reduce(
                        out=dst_, in_=src_.rearrange("p (n k) -> p n k", n=8), axis=AX.X, op=ALU.add),
                        reads=[src_], writes=[dst_])
                kdv = kd[:].rearrange("p (n m) -> p n m", n=8)
                for n in range(8):
                    ts(kdv[:, n, :], kb8[:, :], kb8[:, n:n + 1], None, ALU.subtract)

            bts = {}
            for qc_ in (2, 3):
                for hh_ in range(2):
                    bts[(qc_, hh_)] = biasT[(qc_ - 2) * 2 + hh_][:, :]
            grp_all = [(qb, hh) for qb in range(4, 8) for hh in range(2)]
            bks = {}

            def emit_bias1(g0):
                grp = grp_all[g0:g0 + 4]
                for key in grp:
                    qb, hh = key
                    bks[key] = gbank()
                    mm(bks[key][0:64, 0:256], kd[rows_(hh), :], qb_[rows_(hh), QS_(qb)], True, True)
                for i_, key in enumerate(grp):
                    ts(itile[i_][:, :], bks[key][0:64, 0:256], 0.0, None, ALU.is_gt)

            def emit_bias2(g0):
                grp = grp_all[g0:g0 + 4]
                for i_, key in enumerate(grp):
                    qb, hh = key
                    pr = slice(hh * 64, hh * 64 + 8)
                    tpk = {} if hh == 0 else {'tile_position': (0, 64)}
                    mm(bks[key][pr, 256:512], selc(qb - 4), itile[i_][:, :], True, True, **tpk)
                for i_, key in enumerate(grp):
                    qb, hh = key
                    pr = slice(hh * 64, hh * 64 + 8)
                    half = slice((qb % 2) * 256, (qb % 2 + 1) * 256)
                    ts(biasT[(qb // 2 - 2) * 2 + hh][pr, half], bks[key][pr, 256:512], 2.5, NEG,
                       ALU.is_ge, ALU.mult)

            steps = [(qc, hh, kt) for qc in range(4) for hh in range(2) for kt in range(4 * qc + 4)]
            DSKEW = 3
            pslot = {}
            accb = {}

            def phase1(st):
                qc, hh, kt = st
                rows = rows_(hh)
                n = kt // 2
                j = kt % 2
                bt = bts.get((qc, hh))
                use_mask = n >= 2 * qc
                use_bias = (bt is not None) and (n <= 2 * qc)
                bS = gbank()
                mm(bS[:], kpad[hh][:, kt * 128:(kt + 1) * 128], qb_[:, qc * 512:(qc + 1) * 512],
                   True, not (use_mask or use_bias))
                if use_mask:
                    mm(bS[:], ident, cm4((n - 2 * qc) * 2 + j), False, not use_bias)
                if use_bias:
                    mm(bS[:], ohf(n), bt, False, True)
                p_ = pT[pt_rr[0] % 4]
                pt_rr[0] += 1
                act(p_[:], bS[:], AF.Exp, scale=0.125)
                pslot[st] = p_

            def phase2(st):
                qc, hh, kt = st
                rows = rows_(hh)
                hcol = slice((2 * c + hh) * 64, (2 * c + hh + 1) * 64)
                tp = {} if hh == 0 else {'tile_position': (0, 64)}
                if qc not in accb:
                    accb[qc] = (lbank(), lbank())
                bo, bsum = accb[qc]
                p_ = pslot.pop(st)
                nkt = 4 * qc + 4
                first = (kt == 0)
                lastm = (kt == nkt - 1)
                mm(bo[rows, :], vall[:, kt, hcol], p_[:], first, lastm, **tp)
                mm(bsum[rows, :], ones1, p_[:], first, lastm, **tp)
                if hh == 1 and lastm:
                    rc = rcb[0][:]
                    nt = ntmp[0][:]
                    S.op('dve', lambda h, rc=rc, bsum=bsum: h.reciprocal(out=rc, in_=bsum[:]),
                         reads=[bsum[:]], writes=[rc])
                    tt(nt, bo[:], rc, ALU.mult)
                    act(ya[:, c, qc * 512:(qc + 1) * 512], nt, AF.Identity, bias=bcol(3, c))

            f1 = next(i for i, st in enumerate(steps) if st[0] == 1)
            events = {f1: [lambda: emit_rope([1, 2, 3]), emit_kdiff],
                      f1 + 9: [lambda: emit_bias1(0)], f1 + 14: [lambda: emit_bias2(0)],
                      f1 + 22: [lambda: emit_bias1(4)], f1 + 27: [lambda: emit_bias2(4)]}
            emit_rope([0])
            for i_ in range(len(steps) + DSKEW):
                for ev in events.get(i_, ()):
                    ev()
                if i_ < len(steps):
                    phase1(steps[i_])
                if i_ - DSKEW >= 0:
                    phase2(steps[i_ - DSKEW])

        if stop == 'B':
            return finish([fl(ya), fl(qbuf[0]), fl(kpad[1]), fl(vall)])
        for m in range(8):
            wsg = w_next(l, 8 + m)
            wpr = sm_next(l, 1 + m)
            wprv = wpr[:].rearrange("p (k n) -> p k n", k=10)
            ysrc = [(ya, 0, 4), (yb, 4, 2), (yc, 6, 2), (yd, 8, 2)]
            for tc in range(4):
                ac = macc[tc % 2][:]
                for i in range(4):
                    bgate = gbank()
                    bprj = gbank()
                    inproj(bgate, wsg, i, tc)
                    ysb, k0, nk = ysrc[i]
                    for k in range(nk):
                        mm(bprj[:], wprv[:, k0 + k, :], ysb[:, k, TC(tc)], k == 0, k == nk - 1)
                    sg = msg[i % 3][:]
                    act(sg, bgate[:], AF.Sigmoid, bias=bcol(8 + m, i))
                    if i == 0:
                        stt(ac, sg, INV_ALPHA, bprj[:], ALU.mult, ALU.mult)
                    else:
                        tm = mtm[i % 2][:]
                        stt(tm, sg, INV_ALPHA, bprj[:], ALU.mult, ALU.mult)
                        if i < 3:
                            tt(ac, ac, tm, ALU.add)
                        else:
                            tt(mix[:, m, TC(tc)], ac, tm, ALU.add)

        if stop == 'C':
            return finish([fl(mix)])
        wo0 = w_next(l, 16)
        wo1 = w_next(l, 17, keep=1)
        wov = [wo0[:].rearrange("p (k n) -> p k n", k=4), wo1[:].rearrange("p (k n) -> p k n", k=4)]
        pend = None
        for tc in range(4):
            rr = r32[tc % 2]
            for m in range(8):
                b = gbank()
                for k in range(8):
                    mm(b[:], wov[k // 4][:, k % 4, m * 128:(m + 1) * 128], mix[:, k, TC(tc)], k == 0, False)
                mm(b[:], ident, xhi[:, m, TC(tc)], False, False)
                mm(b[:], ident, xlo[:, m, TC(tc)], False, True)
                act(rr[:, m, :], b[:], AF.Copy)
                act(lsq[:, m, :], b[:], AF.Square)
                cpy(lrb[:, m, :], b[:])
                if pend is not None:
                    ln_apply_chunk(*pend, m)
            sd, nmr = ln_stats(tc % 2, lrb, lsq)
            pend = (l, tc, [rr[:, m, :] for m in range(8)], sd, nmr, 140, 148, False)

        def mlp1_tile(w1, s_, j, tc):
            b = gbank()
            inproj(b, w1, j, tc)
            tm = htm[(j * 4 + tc) % 3][:]
            act(tm, b[:], AF.Relu, scale=INV_ALPHA)
            tt(hbuf[:, s_ * 4 + j, TC(tc)], tm, b[:], ALU.mult)
        w1_first = w_next(l, 18)
        for i_ in range(8):
            mlp1_tile(w1_first, 0, i_ % 4, i_ // 4)
            ln_apply_chunk(*pend, i_)

        if stop == 'D':
            return finish([fl(xhi), fl(xlo)])
        for g in range(4):
            for s in range(2):
                order = [(j, tc) for j in range(4) for tc in range(4)]
                if g == 0 and s == 0:
                    w1 = w1_first
                    order = [(j, tc) for tc in (2, 3) for j in range(4)]
                else:
                    w1 = w_next(l, 18 + 4 * g + s)
                    if g == 0:
                        order = [(j, tc) for tc in range(4) for j in range(4)]
                for j, tc in order:
                    mlp1_tile(w1, s, j, tc)
            for hf in range(2):
                w2 = w_next(l, 18 + 4 * g + 2 + hf)
                w2v = w2[:].rearrange("p (k n) -> p k n", k=8)
                for mq in range(4):
                    m = hf * 4 + mq
                    for tc in range(4):
                        b = gbank()
                        for fc in range(8):
                            mm(b[:], w2v[:, fc, mq * 128:(mq + 1) * 128], hbuf[:, fc, TC(tc)], fc == 0,
                               fc == 7 and g > 0)
                        if g == 0:
                            mm(b[:], ident, xhi[:, m, TC(tc)], False, False)
                            mm(b[:], ident, xlo[:, m, TC(tc)], False, True)
                            act(acc[:, m, TC(tc)], b[:], AF.Copy)
                        else:
                            tt(acc[:, m, TC(tc)], acc[:, m, TC(tc)], b[:], ALU.add)

        last = (l == L - 1)
        pend = None

        def fin(pend):
            ln_apply(*pend)
            if last:
                tcp = pend[1]
                o = dma('sp', outT_d.rearrange("(c p) t -> p c t", p=128)[:, :, TC(tcp)], acc[:, :, TC(tcp)],
                        reads=[acc[:, :, TC(tcp)]])
                out_dmas.append(o)
        for tc in range(4):
            for m in range(8):
                cpy(lrb2[:, m, :], acc[:, m, TC(tc)])
                act(lsq2[:, m, :], acc[:, m, TC(tc)], AF.Square)
            if pend is not None:
                fin(pend)
            sd, nmr = ln_stats(tc % 2, lrb2, lsq2)
            pend = (l, tc, [acc[:, m, TC(tc)] for m in range(8)], sd, nmr, 156, 164, last)
        fin(pend)

    S.emit(final_waits=out_dmas)
    return nc


def _rope_tables():
    inv = (np.float32(10000.0) ** (-np.arange(0, 64, 2, dtype=np.float32) / np.float32(64))).astype(np.float32)
    ang = (np.arange(SEQ, dtype=np.float32)[:, None] * inv[None, :]).astype(np.float32)
    cos = np.cos(ang).astype(np.float32).T
    sin = np.sin(ang).astype(np.float32).T
    r = np.arange(128)
    cosT = cos[r % 32]
    sgn = np.where((r % 64) < 32, -1.0, 1.0).astype(np.float32)[:, None]
    sinT = sin[r % 32] * sgn
    return cosT, sinT


def _constants():
    cosT, sinT = _rope_tables()
    cstf = np.zeros((128, NCF), np.float32)
    cstf[:, 0:SEQ] = cosT
    cstf[:, SEQ:2 * SEQ] = sinT
    wins = (2, 4, 8, 16)
    p = np.arange(128)
    for cc in range(2):
        w = np.array([wins[cc * 2 + (pp // 64)] for pp in p], np.float32)
        cstf[:, 4096 + cc] = 1.0 / w
        for t in range(16):
            cstf[:, 4098 + cc * 16 + t] = 1.0 / np.minimum(t + 1, w)
    cstf[:, 4130] = LN_EPS / (ALPHA * ALPHA)
    cstf[:, 4131] = LN_EPS
    cstb = np.zeros((128, NCB), np.float32)
    cstb[:, 0:128] = np.eye(128, dtype=np.float32)
    cstb[:, 128:256] = 1.0 / 1024.0
    cstb[:, 256:384] = 1.0 / 256.0
    cstb[:, 384:448] = 1.0
    key = np.arange(128)[:, None]
    q = np.arange(128)[None, :]
    cm = np.where(key <= q, 0.0, NEG).astype(np.float32)
    cstb[:, 448:576] = cm
    cstb[:, 576:704] = 0.0
    cstb[:, 704:832] = NEG
    cstb[:, 832:960] = cm
    for n in range(8):
        cstb[n, 960 + n * 128: 960 + (n + 1) * 128] = 1.0
        cstb[64 + n, 960 + n * 128: 960 + (n + 1) * 128] = 1.0
    for qi in range(4):
        qb = 4 + qi
        for n in range(8):
            for m in range(8):
                if m < qb and n < qb:
                    cstb[n * 8 + m, 1984 + qi * 8 + n] = 1.0
    for i in range(4):
        base = 2016 + i * 512
        for qq in range(4):
            blk = cstb[:, base + qq * 128: base + (qq + 1) * 128]
            if qq < i:
                blk[:] = NEG
            elif qq == i:
                blk[:] = cm
            else:
                blk[:] = 0.0
    return cstf, cstb


def _prep_weights(inp, L):
    f = np.float32
    swap = np.arange(512).reshape(8, 2, 32)[:, ::-1, :].reshape(512)
    wbig = np.zeros((L, NSLOT, 128, 4096), f)
    wsm = np.zeros((L, NSM, 128, 1280), f)
    par = np.zeros((128, L, NPAR), f)

    def slot_k8(w):
        return w.reshape(8, 128, 512).transpose(1, 0, 2).reshape(128, 4096)

    for l in range(L):
        w_in = inp['w_in'][l]
        b_in = inp['b_in'][l]
        cols = []
        cols.append(np.concatenate([np.arange(1536, 1792), np.arange(1792, 2048)]))
        cols.append(np.concatenate([np.arange(2048, 2304), np.arange(2304, 2560)]))
        cols.append(np.concatenate([np.arange(2816, 3072), np.arange(2560, 2816)]))
        cols.append(np.arange(1024, 1536))
        for c in range(4):
            qc = np.arange(c * 128, (c + 1) * 128)
            cols.append(np.concatenate([qc, swap[qc], 512 + qc, 512 + swap[qc]]))
        for m in range(8):
            cols.append(np.concatenate([3072 + i * 1024 + np.arange(m * 128, (m + 1) * 128) for i in range(4)]))
        for s, cs in enumerate(cols):
            wbig[l, s] = slot_k8(w_in[:, cs])
            par[:, l, s * 4:(s + 1) * 4] = b_in[cs].reshape(4, 128).T
        wo = inp['w_o'][l]
        for hk in range(2):
            wbig[l, 16 + hk] = wo[hk * 512:(hk + 1) * 512].reshape(4, 128, 1024).transpose(1, 0, 2).reshape(128, 4096)
        w1 = inp['w_mlp1'][l]
        w2 = inp['w_mlp2'][l]
        for g in range(4):
            for s in range(2):
                c0 = (2 * g + s) * 512
                wbig[l, 18 + 4 * g + s] = slot_k8(w1[:, c0:c0 + 512])
            for hf in range(2):
                blk = w2[g * 1024:(g + 1) * 1024, hf * 512:(hf + 1) * 512]
                wbig[l, 18 + 4 * g + 2 + hf] = slot_k8(blk)
        wp = inp['w_pool'][l]
        bd = np.zeros((2, 128, 128), f)
        for g in range(4):
            cc, hh = g // 2, g % 2
            bd[cc, hh * 64:(hh + 1) * 64, hh * 64:(hh + 1) * 64] = wp[g]
        wsm[l, 0, :, 0:256] = bd.transpose(1, 0, 2).reshape(128, 256)
        wpall = np.concatenate([inp['w_pa'][l], inp['w_pb'][l], inp['w_pc'][l], inp['w_pd'][l]], 0)
        for m in range(8):
            wsm[l, 1 + m] = wpall[:, m * 128:(m + 1) * 128].reshape(10, 128, 128).transpose(1, 0, 2).reshape(128, 1280)
        wc = inp['w_dw_conf'][l]
        for cc in range(2):
            par[:, l, 64 + cc * 31: 64 + (cc + 1) * 31] = wc[:, cc * 128:(cc + 1) * 128].T
            par[:, l, 126 + cc] = inp['b_dw_conf'][l][cc * 128:(cc + 1) * 128]
            par[:, l, 128 + cc] = inp['ln_conf_g'][l][cc * 128:(cc + 1) * 128]
            par[:, l, 130 + cc] = inp['ln_conf_b'][l][cc * 128:(cc + 1) * 128]
            par[:, l, 132 + cc] = inp['pool_scale'][l][cc * 128:(cc + 1) * 128]
            par[:, l, 134 + cc * 3: 134 + cc * 3 + 3] = inp['w_sc'][l][:, cc * 128:(cc + 1) * 128].T
        par[:, l, 140:148] = inp['ln1_g'][l].reshape(8, 128).T
        par[:, l, 148:156] = inp['ln1_b'][l].reshape(8, 128).T
        par[:, l, 156:164] = inp['ln2_g'][l].reshape(8, 128).T
        par[:, l, 164:172] = inp['ln2_b'][l].reshape(8, 128).T
    return wbig, wsm, np.ascontiguousarray(par.reshape(128, L * NPAR))


_CACHE = {}


def run(inputs, n_layers=DEPTH, cores=NCORES, debug=False):
    inp = {k: np.asarray(v, dtype=np.float32) for k, v in inputs.items()}
    key = (n_layers, debug)
    nc = build_program(n_layers, debug)
    wbig, wsm, par = _prep_weights(inp, n_layers)
    cstf, cstb = _constants()
    x = inp['x']
    in_maps = []
    for c in range(cores):
        in_maps.append({"xT": np.ascontiguousarray(x[c].T), "wbig": wbig, "wsm": wsm, "par": par,
                        "cstf": cstf, "cstb": cstb})
    res = run_bass_kernel_spmd(nc, in_maps, core_ids=list(range(cores)))
    outs = [np.ascontiguousarray(r["outT"].T) for r in res.results]
    return np.stack(outs, 0), res


def kernel(**inputs):
    out, _ = run(inputs)
    return out.astype(np.float32)
```

```python
import numpy as np
from contextlib import ExitStack
import concourse.bass as bass
import concourse.mybir as mybir
from concourse.bass_utils import run_bass_kernel_spmd

F32 = mybir.dt.float32
BF16 = mybir.dt.bfloat16
AF = mybir.ActivationFunctionType
ALU = mybir.AluOpType
AX = mybir.AxisListType

D_MODEL = 1024
SEQ = 2048
DEPTH = 4
NCORES = 8
ALPHA = (2.0 * DEPTH) ** 0.25
INV_ALPHA = 1.0 / ALPHA
LN_EPS = 1e-5
NEG = -30000.0

PAGE = 256
ENGS = ['pe', 'act', 'dve', 'pool', 'sp']
NDSEM = 8
MAXEPOCH = 8

NSLOT = 34
NSM = 9
NPAR = 176
NCF = 4136
NCB = 4064


class Op:
    __slots__ = ('eng', 'fn', 'deps', 'inc', 'epoch', 'dma', 'tok', 'idx', 'prewait')

    def __init__(self, eng, fn, epoch, dma):
        self.eng = eng
        self.fn = fn
        self.deps = []
        self.inc = False
        self.epoch = epoch
        self.dma = dma
        self.tok = None
        self.prewait = None


class Sched:
    def __init__(self, nc):
        self.nc = nc
        self.ops = {e: [] for e in ENGS}
        self.last_w = {}
        self.readers = {}
        self.epoch = 0
        self.tbase = {}
        self.dma_uses = {}
        self.dma_rr = {e: 0 for e in ENGS}
        self._pcache = {}

    def sb(self, name, shape, dtype, offset):
        t = self.nc.alloc_sbuf_tensor_at(name, list(shape), dtype, offset=offset)
        self.tbase[t.name] = ('sb', offset)
        return t

    def reg_psum(self, t, bank):
        self.tbase[t.name] = ('ps', bank * 2048)

    def keys(self, a):
        t = a.tensor
        ck = (t.name, a.offset, a.ap, a.dtype)
        r = self._pcache.get(ck)
        if r is not None:
            return r
        space, base = self.tbase[t.name]
        isz = mybir.dt.size(a.dtype)
        ap = a.ap
        pstep, pcnt = ap[0]
        if pstep == 0:
            p0 = 0
            foff = a.offset
        else:
            p0 = a.offset // pstep
            foff = a.offset % pstep
        q0, q1 = p0 // 32, (p0 + pcnt - 1) // 32
        dims = [d for d in ap[1:] if d[1] > 1 and d[0] != 0]
        if not dims:
            dims = [(1, 1)]
        inner = dims[-1]
        outer = dims[:-1]
        runlen = (inner[1] - 1) * abs(inner[0]) + 1
        pages = set()
        idx = [0] * len(outer)
        while True:
            st = foff + sum(i * d[0] for i, d in zip(idx, outer))
            b0 = base + st * isz
            b1 = base + (st + runlen) * isz - 1
            for pg in range(b0 // PAGE, b1 // PAGE + 1):
                pages.add(pg)
            k = len(outer) - 1
            while k >= 0:
                idx[k] += 1
                if idx[k] < outer[k][1]:
                    break
                idx[k] = 0
                k -= 1
            if k < 0:
                break
        if space == 'ps':
            banks = set(pg * PAGE // 2048 for pg in pages)
            r = [(space, b, q) for b in banks for q in range(q0, q1 + 1)]
        else:
            r = [(space, pg, q) for pg in pages for q in range(q0, q1 + 1)]
        self._pcache[ck] = r
        return r

    def op(self, eng, fn, reads=(), writes=(), dma=False):
        o = Op(eng, fn, self.epoch, dma)
        deps = set()
        ps_reads = [a for a in reads if self.tbase[a.tensor.name][0] == 'ps']
        if ps_reads:
            reads = [a for a in reads if self.tbase[a.tensor.name][0] != 'ps']
            writes = list(writes) + ps_reads
        for a in reads:
            for k in self.keys(a):
                w = self.last_w.get(k)
                if w is not None:
                    deps.add(w)
                self.readers.setdefault(k, []).append(o)
        for a in writes:
            for k in self.keys(a):
                w = self.last_w.get(k)
                if w is not None:
                    deps.add(w)
                rs = self.readers.get(k)
                if rs:
                    deps.update(rs)
                self.last_w[k] = o
                self.readers[k] = []
        deps.discard(o)
        o.deps = list(deps)
        for d in o.deps:
            if not d.dma and not (d.eng == 'pe' and eng == 'pe'):
                d.inc = True
        o.idx = len(self.ops[eng])
        self.ops[eng].append(o)
        if dma:
            k = self.dma_rr[eng] % NDSEM
            self.dma_rr[eng] += 1
            u = self.dma_uses.get((eng, k), 0)
            o.prewait = ((eng, k), 16 * u)
            self.dma_uses[(eng, k)] = u + 1
            o.tok = (('d', eng, k), 16 * (u + 1))
        return o

    def new_epoch(self):
        self.epoch += 1
        assert self.epoch < MAXEPOCH

    def emit(self, final_waits=()):
        nc = self.nc
        for e in ENGS:
            cnt = {}
            for o in self.ops[e]:
                if o.dma:
                    continue
                if o.inc:
                    c = cnt.get(o.epoch, 0) + 1
                    cnt[o.epoch] = c
                    o.tok = (('e', e, o.epoch), c)
        with ExitStack() as es:
            sems = {}
            for e in ENGS:
                for ep in range(self.epoch + 1):
                    sems[('e', e, ep)] = es.enter_context(nc.semaphore(f"s_{e}_{ep}"))
            for (e, k) in self.dma_uses:
                sems[('d', e, k)] = es.enter_context(nc.semaphore(f"d_{e}_{k}"))
            block = es.enter_context(nc.Block())
            handles = {'pe': block.tensor, 'act': block.scalar, 'dve': block.vector,
                       'pool': block.gpsimd, 'sp': block.sync}

            def make(e):
                def body(h):
                    seen = {}
                    for o in self.ops[e]:
                        toks = []
                        for d in o.deps:
                            if d.eng == e and e == 'pe' and not d.dma:
                                continue
                            toks.append(d.tok)
                        if o.dma:
                            (qe, k), v = o.prewait
                            if v > 0:
                                toks.append((('d', qe, k), v))
                        best = {}
                        for (s, v) in toks:
                            if v > best.get(s, 0):
                                best[s] = v
                        for s, v in best.items():
                            if seen.get(s, 0) >= v:
                                continue
                            h.wait_ge(sems[s], v)
                            seen[s] = v
                        ins = o.fn(h)
                        if o.dma:
                            ins.then_inc(sems[o.tok[0]], 16)
                        elif o.inc:
                            ins.then_inc(sems[o.tok[0]], 1)
                    if e == 'sp':
                        for o in final_waits:
                            s, v = o.tok
                            if seen.get(s, 0) < v:
                                h.wait_ge(sems[s], v)
                                seen[s] = v
                return body
            for e in ENGS:
                handles[e](make(e))


def build_program(n_layers=DEPTH, debug=False, stop=None):
    nc = bass.Bass("TRN2", target_bir_lowering=False)
    S = Sched(nc)
    L = n_layers
    T = SEQ

    xT_d = nc.dram_tensor("xT", [1024, T], F32, kind="ExternalInput").ap()
    wbig_d = nc.dram_tensor("wbig", [L, NSLOT, 128, 4096], F32, kind="ExternalInput").ap()
    wsm_d = nc.dram_tensor("wsm", [L, NSM, 128, 1280], F32, kind="ExternalInput").ap()
    par_d = nc.dram_tensor("par", [128, L * NPAR], F32, kind="ExternalInput").ap()
    cstf_d = nc.dram_tensor("cstf", [128, NCF], F32, kind="ExternalInput").ap()
    cstb_d = nc.dram_tensor("cstb", [128, NCB], F32, kind="ExternalInput").ap()
    outT_d = nc.dram_tensor("outT", [1024, T], F32, kind="ExternalOutput").ap()
    dbg_d = None
    if debug:
        dbg_d = nc.dram_tensor("dbg", [8, 128, 8 * T], F32, kind="ExternalOutput").ap()

    BASE = 16640
    LIMIT = 229344
    cur = [BASE]
    names = [0]

    def alloc(shape, dt, at=None):
        n = int(np.prod(shape[1:])) * mybir.dt.size(dt)
        n = (n + 63) // 64 * 64
        if at is None:
            off = (cur[0] + PAGE - 1) // PAGE * PAGE
            cur[0] = off + n
        else:
            off = at
        assert off + n <= LIMIT, (off, n)
        names[0] += 1
        return S.sb(f"t{names[0]}", shape, dt, off)

    xhi = alloc([128, 8, T], BF16)
    xlo = alloc([128, 8, T], BF16)
    par = alloc([128, L * NPAR], F32)
    csm = alloc([128, 40], F32)
    cb = alloc([128, NCB], BF16)
    wslot = [alloc([128, 4096], BF16) for _ in range(3)]
    wsml = [alloc([128, 1280], BF16) for _ in range(2)]
    D0 = (cur[0] + PAGE - 1) // PAGE * PAGE
    DYN = LIMIT - D0
    assert DYN >= 104900, DYN

    def at(off, shape, dt):
        return alloc(shape, dt, at=D0 + off)

    ident = cb[:, 0:128]
    onesD = cb[:, 128:256]
    ones256 = cb[:, 256:384]
    ones1 = cb[:, 384:448]
    cmA = cb[:, 448:704]
    cmB = cb[:, 704:960]

    def oh(n, hh=0):
        return cb[hh * 64: hh * 64 + 8, 960 + n * 128: 960 + (n + 1) * 128]

    def cm4(i):
        return cb[:, 2016 + i * 512: 2016 + (i + 1) * 512]

    def ohf(n):
        return cb[:, 960 + n * 128: 960 + (n + 1) * 128]

    def selc(q):
        return cb[0:64, 1984 + q * 8: 1984 + (q + 1) * 8]
    invw = csm[:, 0:2]
    f16 = csm[:, 2:34]
    eps_main = csm[:, 34:35]
    eps_conf = csm[:, 35:36]

    ya = at(0, [128, 4, T], BF16)
    yb = at(16384, [128, 2, T], BF16)
    yc = at(24576, [128, 2, T], BF16)
    yd = at(32768, [128, 2, T], BF16)
    cos_t = at(40960, [128, T], F32)
    sin_t = at(49152, [128, T], F32)
    ATT = 57344
    qbuf = [at(ATT, [128, T], BF16)]
    kpad = [at(ATT + 4096 + i * 4096, [128, T], BF16) for i in range(2)]
    vall = at(ATT + 16384, [128, 16, 512], BF16)
    A2 = ATT + 32768
    pT = [at(A2 + i * 1024, [128, 512], BF16) for i in range(4)]
    rcb = [at(A2 + 4096, [128, 512], F32)]
    ntmp = [at(A2 + 6144, [128, 512], F32)]
    rt1 = [at(ATT + 12288, [128, 512], F32)] * 2
    rt2 = [at(ATT + 14336, [128, 512], F32)] * 2
    biasT = [at(A2 + 8192 + i * 1024, [128, 512], BF16) for i in range(4)]
    itile = [at(A2 + 12288 + i * 512, [64, 256], BF16) for i in range(4)]
    kbar = [at(A2 + 14336 + i * 64, [128, 8], F32) for i in range(2)]
    kdiff = [at(A2 + 14464 + i * 128, [128, 64], BF16) for i in range(2)]
    assert A2 + 14720 <= 104900
    SA = 40960
    cub = at(65536, [128, 2, 30 + T], BF16)
    crb = at(73856, [128, 2, 512], BF16)
    csq = at(75904, [128, 2, 512], BF16)
    csg = [at(82048 + i * 2048, [128, 512], F32) for i in range(2)]
    dg = at(86144, [128, 2, 31, 128], BF16)
    cst = [at(77952, [128, 512], F32), at(80000, [128, 512], F32), at(102016, [128, 512], F32)]
    pup = at(SA, [128, 32 + T], F32)
    psA = at(SA + 8320, [128, 32 + T], F32)
    psB = at(SA + 16640, [128, 32 + T], F32)
    pd = at(SA + 24960, [128, T], BF16)
    pd2 = at(SA + 8320, [128, T], BF16)
    pt16 = at(SA + 29056, [128, 16], F32)
    scx = at(SA + 29184, [128, 2, T], F32)
    scp = at(SA + 45568, [128, 2, 2 + T], F32)
    assert SA + 62016 <= 104900
    mix = at(40960, [128, 8, T], BF16)
    msg = [at(73728 + i * 2048, [128, 512], F32) for i in range(3)]
    mtm = [at(79872 + i * 2048, [128, 512], F32) for i in range(2)]
    macc = [at(83968 + i * 2048, [128, 512], F32) for i in range(2)]
    r32 = [at(i * 16384, [128, 8, 512], F32) for i in range(2)]
    lrb = at(73728, [128, 8, 512], BF16)
    lsq = at(81920, [128, 8, 512], BF16)
    lst = [[at(98304 + i * 2048, [128, 512], F32) for i in range(3)],
           [at(90112 + i * 2048, [128, 512], F32) for i in range(3)]]
    acc = at(0, [128, 8, T], F32)
    hbuf = at(65536, [128, 8, T], BF16)
    htm = [at(98304 + i * 2048, [128, 512], F32) for i in range(3)]
    lrb2 = at(65536, [128, 8, 512], BF16)
    lsq2 = at(73728, [128, 8, 512], BF16)

    ps = []
    for i in range(8):
        t = nc.alloc_psum_tensor(f"ps{i}", [128, 512], F32)
        S.reg_psum(t, i)
        ps.append(t)
    g_rr = [0]
    l_rr = [0]

    def gbank():
        b = ps[g_rr[0] % 4]
        g_rr[0] += 1
        return b

    def lbank():
        b = ps[4 + l_rr[0] % 4]
        l_rr[0] += 1
        return b

    def isap(v):
        return not isinstance(v, (int, float)) and v is not None

    def mm(out, lhsT, rhs, start, stop, **kw):
        S.op('pe', lambda h: h.matmul(out, lhsT=lhsT, rhs=rhs, start=start, stop=stop, **kw),
             reads=[lhsT, rhs], writes=[out])

    def act(out, in_, func, bias=None, scale=None, eng='act'):
        rd = [in_]
        kw = {}
        if bias is not None:
            kw['bias'] = bias
            if isap(bias):
                rd.append(bias)
        if scale is not None:
            kw['scale'] = scale
            if isap(scale):
                rd.append(scale)
        S.op('act', lambda h: h.activation(out=out, in_=in_, func=func, **kw), reads=rd, writes=[out])

    def tt(out, in0, in1, op, eng='dve'):
        S.op(eng, lambda h: h.tensor_tensor(out=out, in0=in0, in1=in1, op=op), reads=[in0, in1], writes=[out])

    def ts(out, in0, s1, s2, op0, op1=None, eng='dve'):
        rd = [in0] + [s for s in (s1, s2) if isap(s)]
        if op1 is None:
            S.op(eng, lambda h: h.tensor_scalar(out=out, in0=in0, scalar1=s1, scalar2=None, op0=op0),
                 reads=rd, writes=[out])
        else:
            S.op(eng, lambda h: h.tensor_scalar(out=out, in0=in0, scalar1=s1, scalar2=s2, op0=op0, op1=op1),
                 reads=rd, writes=[out])

    def stt(out, in0, scalar, in1, op0, op1, eng='dve'):
        rd = [in0, in1] + ([scalar] if isap(scalar) else [])
        S.op(eng, lambda h: h.scalar_tensor_tensor(out=out, in0=in0, scalar=scalar, in1=in1, op0=op0, op1=op1),
             reads=rd, writes=[out])

    def cpy(out, in_, eng='dve'):
        S.op(eng, lambda h: h.tensor_copy(out=out, in_=in_), reads=[in_], writes=[out])

    def dma(q, out, in_, reads=(), writes=()):
        return S.op(q, lambda h: h.dma_start(out=out, in_=in_), reads=reads, writes=writes, dma=True)

    def memset(ap, val, eng='dve'):
        S.op(eng, lambda h: h.memset(ap, val), writes=[ap])

    wq = []
    for l in range(L):
        for s in range(NSLOT):
            wq.append((l, s))
    wstate = {'issued': 0, 'cons': 0}

    def w_issue():
        i = wstate['issued']
        if i >= len(wq):
            return
        l, s = wq[i]
        buf = wslot[i % 3]
        dma('pool', buf[:], wbig_d[l, s], writes=[buf[:]])
        wstate['issued'] += 1

    def w_next(l, s, keep=0):
        i = wstate['cons']
        assert wq[i] == (l, s), (wq[i], l, s)
        wstate['cons'] += 1
        while wstate['issued'] < min(i - keep + 3, len(wq)):
            w_issue()
        return wslot[i % 3]

    smq = [(l, s) for l in range(L) for s in range(NSM)]
    smstate = {'issued': 0, 'cons': 0}

    def sm_issue():
        i = smstate['issued']
        if i >= len(smq):
            return
        l, s = smq[i]
        buf = wsml[i % 2]
        dma('pool', buf[:], wsm_d[l, s], writes=[buf[:]])
        smstate['issued'] += 1

    def sm_next(l, s):
        i = smstate['cons']
        assert smq[i] == (l, s)
        smstate['cons'] += 1
        while smstate['issued'] < min(i + 2, len(smq)):
            sm_issue()
        return wsml[i % 2]

    dma('sp', par[:], par_d, writes=[par[:]])
    dma('sp', csm[:, 0:36], cstf_d[:, 4096:4132], writes=[csm[:, 0:36]])
    dma('pool', cb[:], cstb_d, writes=[cb[:]])
    xin = acc
    dma('sp', xin[:], xT_d.rearrange("(c p) t -> p c t", p=128), writes=[xin[:]])
    for c in range(8):
        act(xhi[:, c, :], xin[:, c, :], AF.Copy)
        tt(xlo[:, c, :], xin[:, c, :], xhi[:, c, :], ALU.subtract)
    w_issue()
    w_issue()
    sm_issue()

    def TC(tc):
        return slice(tc * 512, (tc + 1) * 512)

    out_dmas = []

    def ln_stats(pset, rb, sq):
        b1 = lbank()
        b2 = lbank()
        for c in range(8):
            mm(b1[:], onesD, rb[:, c, :], c == 0, c == 7)
        for c in range(8):
            mm(b2[:], onesD, sq[:, c, :], c == 0, c == 7)
        m2, sd, nmr = lst[pset][0][:], lst[pset][1][:], lst[pset][2][:]
        act(m2, b1[:], AF.Square)
        tt(m2, b2[:], m2, ALU.subtract)
        act(sd, m2, AF.Sqrt, bias=eps_main)
        S.op('dve', lambda h: h.reciprocal(out=sd, in_=sd), reads=[sd], writes=[sd])
        stt(nmr, b1[:], -1.0, sd, ALU.mult, ALU.mult)
        return sd, nmr

    def ln_apply(l, tc, rin, sd, nmr, gcol, bcol, last):
        for c in range(8):
            r = rin[c]
            tt(r, r, sd, ALU.mult)
            tt(r, r, nmr, ALU.add)
        for c in range(8):
            r = rin[c]
            g = par[:, l * NPAR + gcol + c: l * NPAR + gcol + c + 1]
            b = par[:, l * NPAR + bcol + c: l * NPAR + bcol + c + 1]
            act(r, r, AF.Identity, bias=b, scale=g)
        if not last:
            for c in range(8):
                act(xhi[:, c, TC(tc)], rin[c], AF.Copy)
            for c in range(8):
                tt(xlo[:, c, TC(tc)], rin[c], xhi[:, c, TC(tc)], ALU.subtract, eng='pool')

    def finish(dumps):
        outs = []
        for i, a in enumerate(dumps):
            n = a.shape[1]
            outs.append(dma('pool', dbg_d[i, 0:a.shape[0], 0:n], a, reads=[a]))
        S.emit(final_waits=outs)
        return nc

    def fl(t):
        a = t[:]
        if len(a.shape) == 3:
            a = a.rearrange("p a b -> p (a b)")
        return a

    for l in range(L):
        if l > 0:
            S.new_epoch()
        P = l * NPAR

        def bcol(slot, j):
            return par[:, P + slot * 4 + j: P + slot * 4 + j + 1]

        def pcol(i):
            return par[:, P + i: P + i + 1]

        def inproj(bank, ws, j, tc):
            wv = ws[:].rearrange("p (k n) -> p k n", k=8)
            for kc in range(8):
                mm(bank[:], wv[:, kc, j * 128:(j + 1) * 128], xhi[:, kc, TC(tc)], kc == 0, kc == 7)

        ws = w_next(l, 0)
        for cc in range(2):
            memset(cub[:, cc, 0:30], 0.0)
        for tc in range(4):
            for cc in range(2):
                ba = gbank()
                bg = gbank()
                inproj(ba, ws, cc, tc)
                inproj(bg, ws, 2 + cc, tc)
                sg = csg[cc][:]
                act(sg, bg[:], AF.Sigmoid, bias=bcol(0, 2 + cc))
                stt(cub[:, cc, 30 + tc * 512: 30 + (tc + 1) * 512], ba[:], bcol(0, cc), sg, ALU.add, ALU.mult)
        for cc in range(2):
            dgv = dg[:, cc, :, :]
            idb = bass.AP(ident.tensor, ident.offset, [list(ident.ap[0]), [0, 31], [1, 128]])
            wc_ = par[:, P + 64 + cc * 31: P + 64 + (cc + 1) * 31]
            wb = bass.AP(wc_.tensor, wc_.offset, [list(wc_.ap[0]), [1, 31], [0, 128]])
            S.op('dve', lambda h, dgv=dgv, idb=idb, wb=wb: h.tensor_tensor(out=dgv, in0=idb, in1=wb, op=ALU.mult),
                 reads=[ident, wc_], writes=[dgv])
        for tc in range(4):
            cbk = []
            for cc in range(2):
                bk = gbank()
                cbk.append(bk)
                for k in range(31):
                    mm(bk[:], dg[:, cc, k, :], cub[:, cc, tc * 512 + k: tc * 512 + k + 512], k == 0, k == 30)
            for cc in range(2):
                act(crb[:, cc, :], cbk[cc][:], AF.Identity, bias=pcol(126 + cc))
                act(csq[:, cc, :], cbk[cc][:], AF.Square, bias=pcol(126 + cc))
            b1 = lbank()
            b2 = lbank()
            for cc in range(2):
                mm(b1[:], ones256, crb[:, cc, :], cc == 0, cc == 1)
            for cc in range(2):
                mm(b2[:], ones256, csq[:, cc, :], cc == 0, cc == 1)
            m2, sd, nmr = cst[0][:], cst[1][:], cst[2][:]
            act(m2, b1[:], AF.Square)
            tt(m2, b2[:], m2, ALU.subtract)
            act(sd, m2, AF.Sqrt, bias=eps_conf)
            S.op('dve', lambda h, sd=sd: h.reciprocal(out=sd, in_=sd), reads=[sd], writes=[sd])
            stt(nmr, b1[:], -1.0, sd, ALU.mult, ALU.mult)
            for cc in range(2):
                tmp = csg[1][:] if cc == 0 else csg[0][:]
                stt(tmp, cbk[cc][:], pcol(126 + cc), sd, ALU.add, ALU.mult)
                tt(tmp, tmp, nmr, ALU.add)
                act(yb[:, cc, TC(tc)], tmp, AF.Silu, bias=pcol(130 + cc), scale=pcol(128 + cc))

        ws6 = w_next(l, 1)
        wpool = sm_next(l, 0)
        wpv = wpool[:, 0:256].rearrange("p (c n) -> p c n", c=2)
        for cc in range(2):
            memset(pup[:, 0:32], 0.0)
            for tc in range(4):
                b = gbank()
                inproj(b, ws6, cc, tc)
                act(pup[:, 32 + tc * 512: 32 + (tc + 1) * 512], b[:], AF.Identity, bias=bcol(1, cc))
            E = 32 + T
            tt(psA[:, 1:E], pup[:, 1:E], pup[:, 0:E - 1], ALU.add)
            tt(psB[:, 3:E], psA[:, 3:E], psA[:, 1:E - 2], ALU.add)
            if cc == 1:
                tt(psA[:, 7:E], psB[:, 7:E], psB[:, 3:E - 4], ALU.add)
                tt(psB[:, 15:E], psA[:, 15:E], psA[:, 7:E - 8], ALU.add)
            cpy(psB[0:64, 32:E], psA[0:64, 32:E])
            pdc = pd if cc == 0 else pd2
            stt(pdc[:, :], psB[:, 32:E], invw[:, cc:cc + 1], pup[:, 32:E], ALU.mult, ALU.subtract)
            tt(pt16[:, :], psB[:, 32:48], f16[:, cc * 16:(cc + 1) * 16], ALU.mult)
            tt(pdc[:, 0:16], pt16[:, :], pup[:, 32:48], ALU.subtract)
        for cc in range(2):
            for tc in range(4):
                b = gbank()
                inproj(b, ws6, 2 + cc, tc)
                act(scx[:, cc, TC(tc)], b[:], AF.Identity, bias=bcol(1, 2 + cc))
        for cc in range(2):
            pdc = pd if cc == 0 else pd2
            for tc in range(4):
                b = gbank()
                mm(b[:], wpv[:, cc, :], pdc[:, TC(tc)], True, True)
                act(yc[:, cc, TC(tc)], b[:], AF.Identity, scale=pcol(132 + cc))
        ws7 = w_next(l, 2)
        for cc in range(2):
            memset(scp[:, cc, 0:2], 0.0)
            for tc in range(4):
                b = gbank()
                inproj(b, ws7, cc, tc)
                stt(scp[:, cc, 2 + tc * 512: 2 + (tc + 1) * 512], b[:], bcol(2, cc), scx[:, cc, TC(tc)],
                    ALU.add, ALU.mult)
            cv = scx[:, cc, :]
            ts(cv, scp[:, cc, 0:T], pcol(134 + cc * 3 + 0), None, ALU.mult)
            stt(cv, scp[:, cc, 1:T + 1], pcol(134 + cc * 3 + 1), cv, ALU.mult, ALU.add)
            stt(cv, scp[:, cc, 2:T + 2], pcol(134 + cc * 3 + 2), cv, ALU.mult, ALU.add)
            for tc in range(4):
                b = gbank()
                inproj(b, ws7, 2 + cc, tc)
                stt(yd[:, cc, TC(tc)], b[:], bcol(2, 2 + cc), scx[:, cc, TC(tc)], ALU.add, ALU.mult)

        if stop == 'A':
            return finish([fl(yb), fl(yc), fl(yd)])
        dma('sp', cos_t[:], cstf_d[:, 0:T], writes=[cos_t[:]])
        dma('sp', sin_t[:], cstf_d[:, T:2 * T], writes=[sin_t[:]])
        wsv = w_next(l, 3)
        wvv = wsv[:].rearrange("p (k n) -> p k n", k=8)
        for tt_ in range(16):
            b = gbank()
            for kc in range(8):
                mm(b[:], xhi[:, kc, tt_ * 128:(tt_ + 1) * 128], wvv[:, kc, :], kc == 0, kc == 7)
            if tt_ % 2 == 0:
                act(vall[:, tt_, :], b[:], AF.Copy)
            else:
                cpy(vall[:, tt_, :], b[:])
        if stop == 'B0':
            return finish([fl(vall)])
        nb_rr = 0
        pt_rr = [0]
        memset(kpad[0][64:128, :], 0.0)
        memset(kpad[1][0:64, :], 0.0)
        for i_ in range(4):
            memset(biasT[i_][:, :], 0.0)
        for c in range(4):
            if stop is not None and stop.startswith('B3') and c == int(stop[2:]):
                return finish([fl(ya), fl(qbuf[0]), fl(kpad[0]), fl(vall)])
            wsp = w_next(l, 4 + c)
            qb_ = qbuf[0]
            kb8 = kbar[c % 2]
            kd = kdiff[c % 2]

            def rows_(hh):
                return slice(hh * 64, (hh + 1) * 64)

            def QS_(qb):
                return slice(qb * 256, (qb + 1) * 256)

            def emit_rope(tcs):
                for tc in tcs:
                    for which in (0, 1):
                        bz = gbank()
                        bs = gbank()
                        inproj(bz, wsp, which * 2, tc)
                        inproj(bs, wsp, which * 2 + 1, tc)
                        t1 = rt1[tc % 2][:]
                        t2 = rt2[tc % 2][:]
                        stt(t1, bz[:], bcol(4 + c, which * 2), cos_t[:, TC(tc)], ALU.add, ALU.mult)
                        stt(t2, bs[:], bcol(4 + c, which * 2 + 1), sin_t[:, TC(tc)], ALU.add, ALU.mult)
                        if which == 0:
                            tt(qb_[:, TC(tc)], t1, t2, ALU.add)
                        else:
                            tt(kpad[0][0:64, TC(tc)], t1[0:64, :], t2[0:64, :], ALU.add)
                            tt(kpad[1][64:128, TC(tc)], t1[64:128, :], t2[64:128, :], ALU.add)

            def emit_kdiff():
                for hh_ in range(2):
                    pr_ = slice(hh_ * 64, (hh_ + 1) * 64)
                    src_ = kpad[hh_][pr_, :]
                    dst_ = kb8[pr_, :]
                    S.op('dve', lambda h, src_=src_, dst_=dst_: h.tensor_reduce(
                        out=dst_, in_=src_.rearrange("p (n k) -> p n k", n=8), axis=AX.X, op=ALU.add),
                        reads=[src_], writes=[dst_])
                kdv = kd[:].rearrange("p (n m) -> p n m", n=8)
                for n in range(8):
                    ts(kdv[:, n, :], kb8[:, :], kb8[:, n:n + 1], None, ALU.subtract)

            bts = {}
            for qc_ in (2, 3):
                for hh_ in range(2):
                    bts[(qc_, hh_)] = biasT[(qc_ - 2) * 2 + hh_][:, :]
            grp_all = [(qb, hh) for qb in range(4, 8) for hh in range(2)]
            bks = {}

            def emit_bias1(g0):
                grp = grp_all[g0:g0 + 4]
                for key in grp:
                    qb, hh = key
                    bks[key] = gbank()
                    mm(bks[key][0:64, 0:256], kd[rows_(hh), :], qb_[rows_(hh), QS_(qb)], True, True)
                for i_, key in enumerate(grp):
                    ts(itile[i_][:, :], bks[key][0:64, 0:256], 0.0, None, ALU.is_gt)

            def emit_bias2(g0):
                grp = grp_all[g0:g0 + 4]
                for i_, key in enumerate(grp):
                    qb, hh = key
                    pr = slice(hh * 64, hh * 64 + 8)
                    tpk = {} if hh == 0 else {'tile_position': (0, 64)}
                    mm(bks[key][pr, 256:512], selc(qb - 4), itile[i_][:, :], True, True, **tpk)
                for i_, key in enumerate(grp):
                    qb, hh = key
                    pr = slice(hh * 64, hh * 64 + 8)
                    half = slice((qb % 2) * 256, (qb % 2 + 1) * 256)
                    ts(biasT[(qb // 2 - 2) * 2 + hh][pr, half], bks[key][pr, 256:512], 2.5, NEG,
                       ALU.is_ge, ALU.mult)

            steps = [(qc, hh, kt) for qc in range(4) for hh in range(2) for kt in range(4 * qc + 4)]
            DSKEW = 3
            pslot = {}
            accb = {}

            def phase1(st):
                qc, hh, kt = st
                rows = rows_(hh)
                n = kt // 2
                j = kt % 2
                bt = bts.get((qc, hh))
                use_mask = n >= 2 * qc
                use_bias = (bt is not None) and (n <= 2 * qc)
                bS = gbank()
                mm(bS[:], kpad[hh][:, kt * 128:(kt + 1) * 128], qb_[:, qc * 512:(qc + 1) * 512],
                   True, not (use_mask or use_bias))
                if use_mask:
                    mm(bS[:], ident, cm4((n - 2 * qc) * 2 + j), False, not use_bias)
                if use_bias:
                    mm(bS[:], ohf(n), bt, False, True)
                p_ = pT[pt_rr[0] % 4]
                pt_rr[0] += 1
                act(p_[:], bS[:], AF.Exp, scale=0.125)
                pslot[st] = p_

            def phase2(st):
                qc, hh, kt = st
                rows = rows_(hh)
                hcol = slice((2 * c + hh) * 64, (2 * c + hh + 1) * 64)
                tp = {} if hh == 0 else {'tile_position': (0, 64)}
                if qc not in accb:
                    accb[qc] = (lbank(), lbank())
                bo, bsum = accb[qc]
                p_ = pslot.pop(st)
                nkt = 4 * qc + 4
                first = (kt == 0)
                lastm = (kt == nkt - 1)
                mm(bo[rows, :], vall[:, kt, hcol], p_[:], first, lastm, **tp)
                mm(bsum[rows, :], ones1, p_[:], first, lastm, **tp)
                if hh == 1 and lastm:
                    rc = rcb[0][:]
                    nt = ntmp[0][:]
                    S.op('dve', lambda h, rc=rc, bsum=bsum: h.reciprocal(out=rc, in_=bsum[:]),
                         reads=[bsum[:]], writes=[rc])
                    tt(nt, bo[:], rc, ALU.mult)
                    act(ya[:, c, qc * 512:(qc + 1) * 512], nt, AF.Identity, bias=bcol(3, c))

            f1 = next(i for i, st in enumerate(steps) if st[0] == 1)
            events = {f1: [lambda: emit_rope([1, 2, 3]), emit_kdiff],
                      f1 + 9: [lambda: emit_bias1(0)], f1 + 14: [lambda: emit_bias2(0)],
                      f1 + 22: [lambda: emit_bias1(4)], f1 + 27: [lambda: emit_bias2(4)]}
            emit_rope([0])
            for i_ in range(len(steps) + DSKEW):
                for ev in events.get(i_, ()):
                    ev()
                if i_ < len(steps):
                    phase1(steps[i_])
                if i_ - DSKEW >= 0:
                    phase2(steps[i_ - DSKEW])

        if stop == 'B':
            return finish([fl(ya), fl(qbuf[0]), fl(kpad[1]), fl(vall)])
        for m in range(8):
            wsg = w_next(l, 8 + m)
            wpr = sm_next(l, 1 + m)
            wprv = wpr[:].rearrange("p (k n) -> p k n", k=10)
            ysrc = [(ya, 0, 4), (yb, 4, 2), (yc, 6, 2), (yd, 8, 2)]
            for tc in range(4):
                ac = macc[tc % 2][:]
                for i in range(4):
                    bgate = gbank()
                    bprj = gbank()
                    inproj(bgate, wsg, i, tc)
                    ysb, k0, nk = ysrc[i]
                    for k in range(nk):
                        mm(bprj[:], wprv[:, k0 + k, :], ysb[:, k, TC(tc)], k == 0, k == nk - 1)
                    sg = msg[i % 3][:]
                    act(sg, bgate[:], AF.Sigmoid, bias=bcol(8 + m, i))
                    if i == 0:
                        stt(ac, sg, INV_ALPHA, bprj[:], ALU.mult, ALU.mult)
                    else:
                        tm = mtm[i % 2][:]
                        stt(tm, sg, INV_ALPHA, bprj[:], ALU.mult, ALU.mult)
                        if i < 3:
                            tt(ac, ac, tm, ALU.add)
                        else:
                            tt(mix[:, m, TC(tc)], ac, tm, ALU.add)

        if stop == 'C':
            return finish([fl(mix)])
        wo0 = w_next(l, 16)
        wo1 = w_next(l, 17, keep=1)
        wov = [wo0[:].rearrange("p (k n) -> p k n", k=4), wo1[:].rearrange("p (k n) -> p k n", k=4)]
        pend = None
        for tc in range(4):
            rr = r32[tc % 2]
            for m in range(8):
                b = gbank()
                for k in range(8):
                    mm(b[:], wov[k // 4][:, k % 4, m * 128:(m + 1) * 128], mix[:, k, TC(tc)], k == 0, False)
                mm(b[:], ident, xhi[:, m, TC(tc)], False, False)
                mm(b[:], ident, xlo[:, m, TC(tc)], False, True)
                act(rr[:, m, :], b[:], AF.Copy)
                act(lsq[:, m, :], b[:], AF.Square)
                cpy(lrb[:, m, :], b[:])
            if pend is not None:
                ln_apply(*pend)
            sd, nmr = ln_stats(tc % 2, lrb, lsq)
            pend = (l, tc, [rr[:, m, :] for m in range(8)], sd, nmr, 140, 148, False)

        def mlp1_tile(w1, s_, j, tc):
            b = gbank()
            inproj(b, w1, j, tc)
            tm = htm[(j * 4 + tc) % 3][:]
            act(tm, b[:], AF.Relu, scale=INV_ALPHA)
            tt(hbuf[:, s_ * 4 + j, TC(tc)], tm, b[:], ALU.mult)
        w1_first = w_next(l, 18)
        for tc in (0, 1):
            for j in range(4):
                mlp1_tile(w1_first, 0, j, tc)
        ln_apply(*pend)

        if stop == 'D':
            return finish([fl(xhi), fl(xlo)])
        for g in range(4):
            for s in range(2):
                order = [(j, tc) for j in range(4) for tc in range(4)]
                if g == 0 and s == 0:
                    w1 = w1_first
                    order = [(j, tc) for tc in (2, 3) for j in range(4)]
                else:
                    w1 = w_next(l, 18 + 4 * g + s)
                    if g == 0:
                        order = [(j, tc) for tc in range(4) for j in range(4)]
                for j, tc in order:
                    mlp1_tile(w1, s, j, tc)
            for hf in range(2):
                w2 = w_next(l, 18 + 4 * g + 2 + hf)
                w2v = w2[:].rearrange("p (k n) -> p k n", k=8)
                for mq in range(4):
                    m = hf * 4 + mq
                    for tc in range(4):
                        b = gbank()
                        for fc in range(8):
                            mm(b[:], w2v[:, fc, mq * 128:(mq + 1) * 128], hbuf[:, fc, TC(tc)], fc == 0,
                               fc == 7 and g > 0)
                        if g == 0:
                            mm(b[:], ident, xhi[:, m, TC(tc)], False, False)
                            mm(b[:], ident, xlo[:, m, TC(tc)], False, True)
                            act(acc[:, m, TC(tc)], b[:], AF.Copy)
                        else:
                            tt(acc[:, m, TC(tc)], acc[:, m, TC(tc)], b[:], ALU.add)

        last = (l == L - 1)
        pend = None

        def fin(pend):
            ln_apply(*pend)
            if last:
                tcp = pend[1]
                o = dma('sp', outT_d.rearrange("(c p) t -> p c t", p=128)[:, :, TC(tcp)], acc[:, :, TC(tcp)],
                        reads=[acc[:, :, TC(tcp)]])
                out_dmas.append(o)
        for tc in range(4):
            for m in range(8):
                cpy(lrb2[:, m, :], acc[:, m, TC(tc)])
                act(lsq2[:, m, :], acc[:, m, TC(tc)], AF.Square)
            if pend is not None:
                fin(pend)
            sd, nmr = ln_stats(tc % 2, lrb2, lsq2)
            pend = (l, tc, [acc[:, m, TC(tc)] for m in range(8)], sd, nmr, 156, 164, last)
        fin(pend)

    S.emit(final_waits=out_dmas)
    return nc


def _rope_tables():
    inv = (np.float32(10000.0) ** (-np.arange(0, 64, 2, dtype=np.float32) / np.float32(64))).astype(np.float32)
    ang = (np.arange(SEQ, dtype=np.float32)[:, None] * inv[None, :]).astype(np.float32)
    cos = np.cos(ang).astype(np.float32).T
    sin = np.sin(ang).astype(np.float32).T
    r = np.arange(128)
    cosT = cos[r % 32]
    sgn = np.where((r % 64) < 32, -1.0, 1.0).astype(np.float32)[:, None]
    sinT = sin[r % 32] * sgn
    return cosT, sinT


def _constants():
    cosT, sinT = _rope_tables()
    cstf = np.zeros((128, NCF), np.float32)
    cstf[:, 0:SEQ] = cosT
    cstf[:, SEQ:2 * SEQ] = sinT
    wins = (2, 4, 8, 16)
    p = np.arange(128)
    for cc in range(2):
        w = np.array([wins[cc * 2 + (pp // 64)] for pp in p], np.float32)
        cstf[:, 4096 + cc] = 1.0 / w
        for t in range(16):
            cstf[:, 4098 + cc * 16 + t] = 1.0 / np.minimum(t + 1, w)
    cstf[:, 4130] = LN_EPS / (ALPHA * ALPHA)
    cstf[:, 4131] = LN_EPS
    cstb = np.zeros((128, NCB), np.float32)
    cstb[:, 0:128] = np.eye(128, dtype=np.float32)
    cstb[:, 128:256] = 1.0 / 1024.0
    cstb[:, 256:384] = 1.0 / 256.0
    cstb[:, 384:448] = 1.0
    key = np.arange(128)[:, None]
    q = np.arange(128)[None, :]
    cm = np.where(key <= q, 0.0, NEG).astype(np.float32)
    cstb[:, 448:576] = cm
    cstb[:, 576:704] = 0.0
    cstb[:, 704:832] = NEG
    cstb[:, 832:960] = cm
    for n in range(8):
        cstb[n, 960 + n * 128: 960 + (n + 1) * 128] = 1.0
        cstb[64 + n, 960 + n * 128: 960 + (n + 1) * 128] = 1.0
    for qi in range(4):
        qb = 4 + qi
        for n in range(8):
            for m in range(8):
                if m < qb and n < qb:
                    cstb[n * 8 + m, 1984 + qi * 8 + n] = 1.0
    for i in range(4):
        base = 2016 + i * 512
        for qq in range(4):
            blk = cstb[:, base + qq * 128: base + (qq + 1) * 128]
            if qq < i:
                blk[:] = NEG
            elif qq == i:
                blk[:] = cm
            else:
                blk[:] = 0.0
    return cstf, cstb


def _prep_weights(inp, L):
    f = np.float32
    swap = np.arange(512).reshape(8, 2, 32)[:, ::-1, :].reshape(512)
    wbig = np.zeros((L, NSLOT, 128, 4096), f)
    wsm = np.zeros((L, NSM, 128, 1280), f)
    par = np.zeros((128, L, NPAR), f)

    def slot_k8(w):
        return w.reshape(8, 128, 512).transpose(1, 0, 2).reshape(128, 4096)

    for l in range(L):
        w_in = inp['w_in'][l]
        b_in = inp['b_in'][l]
        cols = []
        cols.append(np.concatenate([np.arange(1536, 1792), np.arange(1792, 2048)]))
        cols.append(np.concatenate([np.arange(2048, 2304), np.arange(2304, 2560)]))
        cols.append(np.concatenate([np.arange(2816, 3072), np.arange(2560, 2816)]))
        cols.append(np.arange(1024, 1536))
        for c in range(4):
            qc = np.arange(c * 128, (c + 1) * 128)
            cols.append(np.concatenate([qc, swap[qc], 512 + qc, 512 + swap[qc]]))
        for m in range(8):
            cols.append(np.concatenate([3072 + i * 1024 + np.arange(m * 128, (m + 1) * 128) for i in range(4)]))
        for s, cs in enumerate(cols):
            wbig[l, s] = slot_k8(w_in[:, cs])
            par[:, l, s * 4:(s + 1) * 4] = b_in[cs].reshape(4, 128).T
        wo = inp['w_o'][l]
        for hk in range(2):
            wbig[l, 16 + hk] = wo[hk * 512:(hk + 1) * 512].reshape(4, 128, 1024).transpose(1, 0, 2).reshape(128, 4096)
        w1 = inp['w_mlp1'][l]
        w2 = inp['w_mlp2'][l]
        for g in range(4):
            for s in range(2):
                c0 = (2 * g + s) * 512
                wbig[l, 18 + 4 * g + s] = slot_k8(w1[:, c0:c0 + 512])
            for hf in range(2):
                blk = w2[g * 1024:(g + 1) * 1024, hf * 512:(hf + 1) * 512]
                wbig[l, 18 + 4 * g + 2 + hf] = slot_k8(blk)
        wp = inp['w_pool'][l]
        bd = np.zeros((2, 128, 128), f)
        for g in range(4):
            cc, hh = g // 2, g % 2
            bd[cc, hh * 64:(hh + 1) * 64, hh * 64:(hh + 1) * 64] = wp[g]
        wsm[l, 0, :, 0:256] = bd.transpose(1, 0, 2).reshape(128, 256)
        wpall = np.concatenate([inp['w_pa'][l], inp['w_pb'][l], inp['w_pc'][l], inp['w_pd'][l]], 0)
        for m in range(8):
            wsm[l, 1 + m] = wpall[:, m * 128:(m + 1) * 128].reshape(10, 128, 128).transpose(1, 0, 2).reshape(128, 1280)
        wc = inp['w_dw_conf'][l]
        for cc in range(2):
            par[:, l, 64 + cc * 31: 64 + (cc + 1) * 31] = wc[:, cc * 128:(cc + 1) * 128].T
            par[:, l, 126 + cc] = inp['b_dw_conf'][l][cc * 128:(cc + 1) * 128]
            par[:, l, 128 + cc] = inp['ln_conf_g'][l][cc * 128:(cc + 1) * 128]
            par[:, l, 130 + cc] = inp['ln_conf_b'][l][cc * 128:(cc + 1) * 128]
            par[:, l, 132 + cc] = inp['pool_scale'][l][cc * 128:(cc + 1) * 128]
            par[:, l, 134 + cc * 3: 134 + cc * 3 + 3] = inp['w_sc'][l][:, cc * 128:(cc + 1) * 128].T
        par[:, l, 140:148] = inp['ln1_g'][l].reshape(8, 128).T
        par[:, l, 148:156] = inp['ln1_b'][l].reshape(8, 128).T
        par[:, l, 156:164] = inp['ln2_g'][l].reshape(8, 128).T
        par[:, l, 164:172] = inp['ln2_b'][l].reshape(8, 128).T
    return wbig, wsm, np.ascontiguousarray(par.reshape(128, L * NPAR))


_CACHE = {}


def run(inputs, n_layers=DEPTH, cores=NCORES, debug=False):
    inp = {k: np.asarray(v, dtype=np.float32) for k, v in inputs.items()}
    key = (n_layers, debug)
    nc = build_program(n_layers, debug)
    wbig, wsm, par = _prep_weights(inp, n_layers)
    cstf, cstb = _constants()
    x = inp['x']
    in_maps = []
    for c in range(cores):
        in_maps.append({"xT": np.ascontiguousarray(x[c].T), "wbig": wbig, "wsm": wsm, "par": par,
                        "cstf": cstf, "cstb": cstb})
    res = run_bass_kernel_spmd(nc, in_maps, core_ids=list(range(cores)))
    outs = [np.ascontiguousarray(r["outT"].T) for r in res.results]
    return np.stack(outs, 0), res


def kernel(**inputs):
    out, _ = run(inputs)
    return out.astype(np.float32)
```

```python
import numpy as np
from contextlib import ExitStack
import concourse.bass as bass
import concourse.mybir as mybir
from concourse.bass_utils import run_bass_kernel_spmd

F32 = mybir.dt.float32
BF16 = mybir.dt.bfloat16
AF = mybir.ActivationFunctionType
ALU = mybir.AluOpType
AX = mybir.AxisListType

D_MODEL = 1024
SEQ = 2048
DEPTH = 4
NCORES = 8
ALPHA = (2.0 * DEPTH) ** 0.25
INV_ALPHA = 1.0 / ALPHA
LN_EPS = 1e-5
NEG = -30000.0

PAGE = 256
ENGS = ['pe', 'act', 'dve', 'pool', 'sp']
NDSEM = 8
MAXEPOCH = 8

NSLOT = 34
NSM = 9
NPAR = 176
NCF = 4136
NCB = 4064


class Op:
    __slots__ = ('eng', 'fn', 'deps', 'inc', 'epoch', 'dma', 'tok', 'idx', 'prewait')

    def __init__(self, eng, fn, epoch, dma):
        self.eng = eng
        self.fn = fn
        self.deps = []
        self.inc = False
        self.epoch = epoch
        self.dma = dma
        self.tok = None
        self.prewait = None


class Sched:
    def __init__(self, nc):
        self.nc = nc
        self.ops = {e: [] for e in ENGS}
        self.last_w = {}
        self.readers = {}
        self.epoch = 0
        self.tbase = {}
        self.dma_uses = {}
        self.dma_rr = {e: 0 for e in ENGS}
        self._pcache = {}

    def sb(self, name, shape, dtype, offset):
        t = self.nc.alloc_sbuf_tensor_at(name, list(shape), dtype, offset=offset)
        self.tbase[t.name] = ('sb', offset)
        return t

    def reg_psum(self, t, bank):
        self.tbase[t.name] = ('ps', bank * 2048)

    def keys(self, a):
        t = a.tensor
        ck = (t.name, a.offset, a.ap, a.dtype)
        r = self._pcache.get(ck)
        if r is not None:
            return r
        space, base = self.tbase[t.name]
        isz = mybir.dt.size(a.dtype)
        ap = a.ap
        pstep, pcnt = ap[0]
        if pstep == 0:
            p0 = 0
            foff = a.offset
        else:
            p0 = a.offset // pstep
            foff = a.offset % pstep
        q0, q1 = p0 // 32, (p0 + pcnt - 1) // 32
        dims = [d for d in ap[1:] if d[1] > 1 and d[0] != 0]
        if not dims:
            dims = [(1, 1)]
        inner = dims[-1]
        outer = dims[:-1]
        runlen = (inner[1] - 1) * abs(inner[0]) + 1
        pages = set()
        idx = [0] * len(outer)
        while True:
            st = foff + sum(i * d[0] for i, d in zip(idx, outer))
            b0 = base + st * isz
            b1 = base + (st + runlen) * isz - 1
            for pg in range(b0 // PAGE, b1 // PAGE + 1):
                pages.add(pg)
            k = len(outer) - 1
            while k >= 0:
                idx[k] += 1
                if idx[k] < outer[k][1]:
                    break
                idx[k] = 0
                k -= 1
            if k < 0:
                break
        if space == 'ps':
            banks = set(pg * PAGE // 2048 for pg in pages)
            r = [(space, b, q) for b in banks for q in range(q0, q1 + 1)]
        else:
            r = [(space, pg, q) for pg in pages for q in range(q0, q1 + 1)]
        self._pcache[ck] = r
        return r

    def op(self, eng, fn, reads=(), writes=(), dma=False):
        o = Op(eng, fn, self.epoch, dma)
        deps = set()
        ps_reads = [a for a in reads if self.tbase[a.tensor.name][0] == 'ps']
        if ps_reads:
            reads = [a for a in reads if self.tbase[a.tensor.name][0] != 'ps']
            writes = list(writes) + ps_reads
        for a in reads:
            for k in self.keys(a):
                w = self.last_w.get(k)
                if w is not None:
                    deps.add(w)
                self.readers.setdefault(k, []).append(o)
        for a in writes:
            for k in self.keys(a):
                w = self.last_w.get(k)
                if w is not None:
                    deps.add(w)
                rs = self.readers.get(k)
                if rs:
                    deps.update(rs)
                self.last_w[k] = o
                self.readers[k] = []
        deps.discard(o)
        o.deps = list(deps)
        for d in o.deps:
            if not d.dma and not (d.eng == 'pe' and eng == 'pe'):
                d.inc = True
        o.idx = len(self.ops[eng])
        self.ops[eng].append(o)
        if dma:
            k = self.dma_rr[eng] % NDSEM
            self.dma_rr[eng] += 1
            u = self.dma_uses.get((eng, k), 0)
            o.prewait = ((eng, k), 16 * u)
            self.dma_uses[(eng, k)] = u + 1
            o.tok = (('d', eng, k), 16 * (u + 1))
        return o

    def new_epoch(self):
        self.epoch += 1
        assert self.epoch < MAXEPOCH

    def emit(self, final_waits=()):
        nc = self.nc
        for e in ENGS:
            cnt = {}
            for o in self.ops[e]:
                if o.dma:
                    continue
                if o.inc:
                    c = cnt.get(o.epoch, 0) + 1
                    cnt[o.epoch] = c
                    o.tok = (('e', e, o.epoch), c)
        with ExitStack() as es:
            sems = {}
            for e in ENGS:
                for ep in range(self.epoch + 1):
                    sems[('e', e, ep)] = es.enter_context(nc.semaphore(f"s_{e}_{ep}"))
            for (e, k) in self.dma_uses:
                sems[('d', e, k)] = es.enter_context(nc.semaphore(f"d_{e}_{k}"))
            block = es.enter_context(nc.Block())
            handles = {'pe': block.tensor, 'act': block.scalar, 'dve': block.vector,
                       'pool': block.gpsimd, 'sp': block.sync}

            def make(e):
                def body(h):
                    seen = {}
                    for o in self.ops[e]:
                        toks = []
                        for d in o.deps:
                            if d.eng == e and e == 'pe' and not d.dma:
                                continue
                            toks.append(d.tok)
                        if o.dma:
                            (qe, k), v = o.prewait
                            if v > 0:
                                toks.append((('d', qe, k), v))
                        best = {}
                        for (s, v) in toks:
                            if v > best.get(s, 0):
                                best[s] = v
                        for s, v in best.items():
                            if seen.get(s, 0) >= v:
                                continue
                            h.wait_ge(sems[s], v)
                            seen[s] = v
                        ins = o.fn(h)
                        if o.dma:
                            ins.then_inc(sems[o.tok[0]], 16)
                        elif o.inc:
                            ins.then_inc(sems[o.tok[0]], 1)
                    if e == 'sp':
                        for o in final_waits:
                            s, v = o.tok
                            if seen.get(s, 0) < v:
                                h.wait_ge(sems[s], v)
                                seen[s] = v
                return body
            for e in ENGS:
                handles[e](make(e))


def build_program(n_layers=DEPTH, debug=False, stop=None):
    nc = bass.Bass("TRN2", target_bir_lowering=False)
    S = Sched(nc)
    L = n_layers
    T = SEQ

    xT_d = nc.dram_tensor("xT", [1024, T], F32, kind="ExternalInput").ap()
    wbig_d = nc.dram_tensor("wbig", [L, NSLOT, 128, 4096], F32, kind="ExternalInput").ap()
    wsm_d = nc.dram_tensor("wsm", [L, NSM, 128, 1280], F32, kind="ExternalInput").ap()
    par_d = nc.dram_tensor("par", [128, L * NPAR], F32, kind="ExternalInput").ap()
    cstf_d = nc.dram_tensor("cstf", [128, NCF], F32, kind="ExternalInput").ap()
    cstb_d = nc.dram_tensor("cstb", [128, NCB], F32, kind="ExternalInput").ap()
    outT_d = nc.dram_tensor("outT", [1024, T], F32, kind="ExternalOutput").ap()
    dbg_d = None
    if debug:
        dbg_d = nc.dram_tensor("dbg", [8, 128, 8 * T], F32, kind="ExternalOutput").ap()

    BASE = 16640
    LIMIT = 229344
    cur = [BASE]
    names = [0]

    def alloc(shape, dt, at=None):
        n = int(np.prod(shape[1:])) * mybir.dt.size(dt)
        n = (n + 63) // 64 * 64
        if at is None:
            off = (cur[0] + PAGE - 1) // PAGE * PAGE
            cur[0] = off + n
        else:
            off = at
        assert off + n <= LIMIT, (off, n)
        names[0] += 1
        return S.sb(f"t{names[0]}", shape, dt, off)

    xhi = alloc([128, 8, T], BF16)
    xlo = alloc([128, 8, T], BF16)
    par = alloc([128, L * NPAR], F32)
    csm = alloc([128, 40], F32)
    cb = alloc([128, NCB], BF16)
    wslot = [alloc([128, 4096], BF16) for _ in range(3)]
    wsml = [alloc([128, 1280], BF16) for _ in range(2)]
    D0 = (cur[0] + PAGE - 1) // PAGE * PAGE
    DYN = LIMIT - D0
    assert DYN >= 104900, DYN

    def at(off, shape, dt):
        return alloc(shape, dt, at=D0 + off)

    ident = cb[:, 0:128]
    onesD = cb[:, 128:256]
    ones256 = cb[:, 256:384]
    ones1 = cb[:, 384:448]
    cmA = cb[:, 448:704]
    cmB = cb[:, 704:960]

    def oh(n, hh=0):
        return cb[hh * 64: hh * 64 + 8, 960 + n * 128: 960 + (n + 1) * 128]

    def cm4(i):
        return cb[:, 2016 + i * 512: 2016 + (i + 1) * 512]

    def ohf(n):
        return cb[:, 960 + n * 128: 960 + (n + 1) * 128]

    def selc(q):
        return cb[0:64, 1984 + q * 8: 1984 + (q + 1) * 8]
    invw = csm[:, 0:2]
    f16 = csm[:, 2:34]
    eps_main = csm[:, 34:35]
    eps_conf = csm[:, 35:36]

    ya = at(0, [128, 4, T], BF16)
    yb = at(16384, [128, 2, T], BF16)
    yc = at(24576, [128, 2, T], BF16)
    yd = at(32768, [128, 2, T], BF16)
    cos_t = at(40960, [128, T], F32)
    sin_t = at(49152, [128, T], F32)
    ATT = 57344
    qbuf = [at(ATT, [128, T], BF16)]
    kpad = [at(ATT + 4096 + i * 4096, [128, T], BF16) for i in range(2)]
    vall = at(ATT + 16384, [128, 16, 512], BF16)
    A2 = ATT + 32768
    pT = [at(A2 + i * 1024, [128, 512], BF16) for i in range(4)]
    rcb = [at(A2 + 4096, [128, 512], F32)]
    ntmp = [at(A2 + 6144, [128, 512], F32)]
    rt1 = [at(ATT + 12288, [128, 512], F32)] * 2
    rt2 = [at(ATT + 14336, [128, 512], F32)] * 2
    biasT = [at(A2 + 8192 + i * 1024, [128, 512], BF16) for i in range(4)]
    itile = [at(A2 + 12288 + i * 512, [64, 256], BF16) for i in range(4)]
    kbar = [at(A2 + 14336 + i * 64, [128, 8], F32) for i in range(2)]
    kdiff = [at(A2 + 14464 + i * 128, [128, 64], BF16) for i in range(2)]
    assert A2 + 14720 <= 104900
    SA = 40960
    cub = at(65536, [128, 2, 30 + T], BF16)
    crb = at(73856, [128, 2, 512], BF16)
    csq = at(75904, [128, 2, 512], BF16)
    csg = [at(82048 + i * 2048, [128, 512], F32) for i in range(2)]
    dg = at(86144, [128, 2, 31, 128], BF16)
    cst = [at(77952, [128, 512], F32), at(80000, [128, 512], F32), at(102016, [128, 512], F32)]
    pup = at(SA, [128, 32 + T], F32)
    psA = at(SA + 8320, [128, 32 + T], F32)
    psB = at(SA + 16640, [128, 32 + T], F32)
    pd = at(SA + 24960, [128, T], BF16)
    pd2 = at(SA + 8320, [128, T], BF16)
    pt16 = at(SA + 29056, [128, 16], F32)
    scx = at(SA + 29184, [128, 2, T], F32)
    scp = at(SA + 45568, [128, 2, 2 + T], F32)
    assert SA + 62016 <= 104900
    mix = at(40960, [128, 8, T], BF16)
    msg = [at(73728 + i * 2048, [128, 512], F32) for i in range(3)]
    mtm = [at(79872 + i * 2048, [128, 512], F32) for i in range(2)]
    macc = [at(83968 + i * 2048, [128, 512], F32) for i in range(2)]
    r32 = [at(i * 16384, [128, 8, 512], F32) for i in range(2)]
    lrb = at(73728, [128, 8, 512], BF16)
    lsq = at(81920, [128, 8, 512], BF16)
    lst = [[at(98304 + i * 2048, [128, 512], F32) for i in range(3)],
           [at(90112 + i * 2048, [128, 512], F32) for i in range(3)]]
    acc = at(0, [128, 8, T], F32)
    hbuf = at(65536, [128, 8, T], BF16)
    htm = [at(98304 + i * 2048, [128, 512], F32) for i in range(3)]
    lrb2 = at(65536, [128, 8, 512], BF16)
    lsq2 = at(73728, [128, 8, 512], BF16)

    ps = []
    for i in range(8):
        t = nc.alloc_psum_tensor(f"ps{i}", [128, 512], F32)
        S.reg_psum(t, i)
        ps.append(t)
    g_rr = [0]
    l_rr = [0]

    def gbank():
        b = ps[g_rr[0] % 4]
        g_rr[0] += 1
        return b

    def lbank():
        b = ps[4 + l_rr[0] % 4]
        l_rr[0] += 1
        return b

    def isap(v):
        return not isinstance(v, (int, float)) and v is not None

    def mm(out, lhsT, rhs, start, stop, **kw):
        S.op('pe', lambda h: h.matmul(out, lhsT=lhsT, rhs=rhs, start=start, stop=stop, **kw),
             reads=[lhsT, rhs], writes=[out])

    def act(out, in_, func, bias=None, scale=None, eng='act'):
        rd = [in_]
        kw = {}
        if bias is not None:
            kw['bias'] = bias
            if isap(bias):
                rd.append(bias)
        if scale is not None:
            kw['scale'] = scale
            if isap(scale):
                rd.append(scale)
        S.op('act', lambda h: h.activation(out=out, in_=in_, func=func, **kw), reads=rd, writes=[out])

    def tt(out, in0, in1, op, eng='dve'):
        S.op(eng, lambda h: h.tensor_tensor(out=out, in0=in0, in1=in1, op=op), reads=[in0, in1], writes=[out])

    def ts(out, in0, s1, s2, op0, op1=None, eng='dve'):
        rd = [in0] + [s for s in (s1, s2) if isap(s)]
        if op1 is None:
            S.op(eng, lambda h: h.tensor_scalar(out=out, in0=in0, scalar1=s1, scalar2=None, op0=op0),
                 reads=rd, writes=[out])
        else:
            S.op(eng, lambda h: h.tensor_scalar(out=out, in0=in0, scalar1=s1, scalar2=s2, op0=op0, op1=op1),
                 reads=rd, writes=[out])

    def stt(out, in0, scalar, in1, op0, op1, eng='dve'):
        rd = [in0, in1] + ([scalar] if isap(scalar) else [])
        S.op(eng, lambda h: h.scalar_tensor_tensor(out=out, in0=in0, scalar=scalar, in1=in1, op0=op0, op1=op1),
             reads=rd, writes=[out])

    def cpy(out, in_, eng='dve'):
        S.op(eng, lambda h: h.tensor_copy(out=out, in_=in_), reads=[in_], writes=[out])

    def dma(q, out, in_, reads=(), writes=()):
        return S.op(q, lambda h: h.dma_start(out=out, in_=in_), reads=reads, writes=writes, dma=True)

    def memset(ap, val, eng='dve'):
        S.op(eng, lambda h: h.memset(ap, val), writes=[ap])

    wq = []
    for l in range(L):
        for s in range(NSLOT):
            wq.append((l, s))
    wstate = {'issued': 0, 'cons': 0}

    def w_issue():
        i = wstate['issued']
        if i >= len(wq):
            return
        l, s = wq[i]
        buf = wslot[i % 3]
        dma('pool', buf[:], wbig_d[l, s], writes=[buf[:]])
        wstate['issued'] += 1

    def w_next(l, s, keep=0):
        i = wstate['cons']
        assert wq[i] == (l, s), (wq[i], l, s)
        wstate['cons'] += 1
        while wstate['issued'] < min(i - keep + 3, len(wq)):
            w_issue()
        return wslot[i % 3]

    smq = [(l, s) for l in range(L) for s in range(NSM)]
    smstate = {'issued': 0, 'cons': 0}

    def sm_issue():
        i = smstate['issued']
        if i >= len(smq):
            return
        l, s = smq[i]
        buf = wsml[i % 2]
        dma('pool', buf[:], wsm_d[l, s], writes=[buf[:]])
        smstate['issued'] += 1

    def sm_next(l, s):
        i = smstate['cons']
        assert smq[i] == (l, s)
        smstate['cons'] += 1
        while smstate['issued'] < min(i + 2, len(smq)):
            sm_issue()
        return wsml[i % 2]

    dma('sp', par[:], par_d, writes=[par[:]])
    dma('sp', csm[:, 0:36], cstf_d[:, 4096:4132], writes=[csm[:, 0:36]])
    dma('pool', cb[:], cstb_d, writes=[cb[:]])
    xin = acc
    dma('sp', xin[:], xT_d.rearrange("(c p) t -> p c t", p=128), writes=[xin[:]])
    for c in range(8):
        act(xhi[:, c, :], xin[:, c, :], AF.Copy)
        tt(xlo[:, c, :], xin[:, c, :], xhi[:, c, :], ALU.subtract)
    w_issue()
    w_issue()
    sm_issue()

    def TC(tc):
        return slice(tc * 512, (tc + 1) * 512)

    out_dmas = []

    def ln_stats(pset, rb, sq):
        b1 = lbank()
        b2 = lbank()
        for c in range(8):
            mm(b1[:], onesD, rb[:, c, :], c == 0, c == 7)
        for c in range(8):
            mm(b2[:], onesD, sq[:, c, :], c == 0, c == 7)
        m2, sd, nmr = lst[pset][0][:], lst[pset][1][:], lst[pset][2][:]
        act(m2, b1[:], AF.Square)
        tt(m2, b2[:], m2, ALU.subtract)
        act(sd, m2, AF.Sqrt, bias=eps_main)
        S.op('dve', lambda h: h.reciprocal(out=sd, in_=sd), reads=[sd], writes=[sd])
        stt(nmr, b1[:], -1.0, sd, ALU.mult, ALU.mult)
        return sd, nmr

    def ln_apply(l, tc, rin, sd, nmr, gcol, bcol, last):
        for c in range(8):
            r = rin[c]
            tt(r, r, sd, ALU.mult)
            tt(r, r, nmr, ALU.add)
        for c in range(8):
            r = rin[c]
            g = par[:, l * NPAR + gcol + c: l * NPAR + gcol + c + 1]
            b = par[:, l * NPAR + bcol + c: l * NPAR + bcol + c + 1]
            act(r, r, AF.Identity, bias=b, scale=g)
        if not last:
            for c in range(8):
                act(xhi[:, c, TC(tc)], rin[c], AF.Copy)
            for c in range(8):
                tt(xlo[:, c, TC(tc)], rin[c], xhi[:, c, TC(tc)], ALU.subtract, eng='pool')

    def finish(dumps):
        outs = []
        for i, a in enumerate(dumps):
            n = a.shape[1]
            outs.append(dma('pool', dbg_d[i, 0:a.shape[0], 0:n], a, reads=[a]))
        S.emit(final_waits=outs)
        return nc

    def fl(t):
        a = t[:]
        if len(a.shape) == 3:
            a = a.rearrange("p a b -> p (a b)")
        return a

    for l in range(L):
        if l > 0:
            S.new_epoch()
        P = l * NPAR

        def bcol(slot, j):
            return par[:, P + slot * 4 + j: P + slot * 4 + j + 1]

        def pcol(i):
            return par[:, P + i: P + i + 1]

        def inproj(bank, ws, j, tc):
            wv = ws[:].rearrange("p (k n) -> p k n", k=8)
            for kc in range(8):
                mm(bank[:], wv[:, kc, j * 128:(j + 1) * 128], xhi[:, kc, TC(tc)], kc == 0, kc == 7)

        ws = w_next(l, 0)
        for cc in range(2):
            memset(cub[:, cc, 0:30], 0.0)
        for tc in range(4):
            for cc in range(2):
                ba = gbank()
                bg = gbank()
                inproj(ba, ws, cc, tc)
                inproj(bg, ws, 2 + cc, tc)
                sg = csg[cc][:]
                act(sg, bg[:], AF.Sigmoid, bias=bcol(0, 2 + cc))
                stt(cub[:, cc, 30 + tc * 512: 30 + (tc + 1) * 512], ba[:], bcol(0, cc), sg, ALU.add, ALU.mult)
        for cc in range(2):
            dgv = dg[:, cc, :, :]
            idb = bass.AP(ident.tensor, ident.offset, [list(ident.ap[0]), [0, 31], [1, 128]])
            wc_ = par[:, P + 64 + cc * 31: P + 64 + (cc + 1) * 31]
            wb = bass.AP(wc_.tensor, wc_.offset, [list(wc_.ap[0]), [1, 31], [0, 128]])
            S.op('dve', lambda h, dgv=dgv, idb=idb, wb=wb: h.tensor_tensor(out=dgv, in0=idb, in1=wb, op=ALU.mult),
                 reads=[ident, wc_], writes=[dgv])
        for tc in range(4):
            cbk = []
            for cc in range(2):
                bk = gbank()
                cbk.append(bk)
                for k in range(31):
                    mm(bk[:], dg[:, cc, k, :], cub[:, cc, tc * 512 + k: tc * 512 + k + 512], k == 0, k == 30)
            for cc in range(2):
                act(crb[:, cc, :], cbk[cc][:], AF.Identity, bias=pcol(126 + cc))
                act(csq[:, cc, :], cbk[cc][:], AF.Square, bias=pcol(126 + cc))
            b1 = lbank()
            b2 = lbank()
            for cc in range(2):
                mm(b1[:], ones256, crb[:, cc, :], cc == 0, cc == 1)
            for cc in range(2):
                mm(b2[:], ones256, csq[:, cc, :], cc == 0, cc == 1)
            m2, sd, nmr = cst[0][:], cst[1][:], cst[2][:]
            act(m2, b1[:], AF.Square)
            tt(m2, b2[:], m2, ALU.subtract)
            act(sd, m2, AF.Sqrt, bias=eps_conf)
            S.op('dve', lambda h, sd=sd: h.reciprocal(out=sd, in_=sd), reads=[sd], writes=[sd])
            stt(nmr, b1[:], -1.0, sd, ALU.mult, ALU.mult)
            for cc in range(2):
                tmp = csg[1][:] if cc == 0 else csg[0][:]
                stt(tmp, cbk[cc][:], pcol(126 + cc), sd, ALU.add, ALU.mult)
                tt(tmp, tmp, nmr, ALU.add)
                act(yb[:, cc, TC(tc)], tmp, AF.Silu, bias=pcol(130 + cc), scale=pcol(128 + cc))

        ws6 = w_next(l, 1)
        wpool = sm_next(l, 0)
        wpv = wpool[:, 0:256].rearrange("p (c n) -> p c n", c=2)
        for cc in range(2):
            memset(pup[:, 0:32], 0.0)
            for tc in range(4):
                b = gbank()
                inproj(b, ws6, cc, tc)
                act(pup[:, 32 + tc * 512: 32 + (tc + 1) * 512], b[:], AF.Identity, bias=bcol(1, cc))
            E = 32 + T
            tt(psA[:, 1:E], pup[:, 1:E], pup[:, 0:E - 1], ALU.add)
            tt(psB[:, 3:E], psA[:, 3:E], psA[:, 1:E - 2], ALU.add)
            if cc == 1:
                tt(psA[:, 7:E], psB[:, 7:E], psB[:, 3:E - 4], ALU.add)
                tt(psB[:, 15:E], psA[:, 15:E], psA[:, 7:E - 8], ALU.add)
            cpy(psB[0:64, 32:E], psA[0:64, 32:E])
            pdc = pd if cc == 0 else pd2
            stt(pdc[:, :], psB[:, 32:E], invw[:, cc:cc + 1], pup[:, 32:E], ALU.mult, ALU.subtract)
            tt(pt16[:, :], psB[:, 32:48], f16[:, cc * 16:(cc + 1) * 16], ALU.mult)
            tt(pdc[:, 0:16], pt16[:, :], pup[:, 32:48], ALU.subtract)
        for cc in range(2):
            for tc in range(4):
                b = gbank()
                inproj(b, ws6, 2 + cc, tc)
                act(scx[:, cc, TC(tc)], b[:], AF.Identity, bias=bcol(1, 2 + cc))
        for cc in range(2):
            pdc = pd if cc == 0 else pd2
            for tc in range(4):
                b = gbank()
                mm(b[:], wpv[:, cc, :], pdc[:, TC(tc)], True, True)
                act(yc[:, cc, TC(tc)], b[:], AF.Identity, scale=pcol(132 + cc))
        ws7 = w_next(l, 2)
        for cc in range(2):
            memset(scp[:, cc, 0:2], 0.0)
            for tc in range(4):
                b = gbank()
                inproj(b, ws7, cc, tc)
                stt(scp[:, cc, 2 + tc * 512: 2 + (tc + 1) * 512], b[:], bcol(2, cc), scx[:, cc, TC(tc)],
                    ALU.add, ALU.mult)
            cv = scx[:, cc, :]
            ts(cv, scp[:, cc, 0:T], pcol(134 + cc * 3 + 0), None, ALU.mult)
            stt(cv, scp[:, cc, 1:T + 1], pcol(134 + cc * 3 + 1), cv, ALU.mult, ALU.add)
            stt(cv, scp[:, cc, 2:T + 2], pcol(134 + cc * 3 + 2), cv, ALU.mult, ALU.add)
        for cc in range(2):
            for tc in range(4):
                b = gbank()
                inproj(b, ws7, 2 + cc, tc)
                stt(yd[:, cc, TC(tc)], b[:], bcol(2, 2 + cc), scx[:, cc, TC(tc)], ALU.add, ALU.mult)

        if stop == 'A':
            return finish([fl(yb), fl(yc), fl(yd)])
        dma('sp', cos_t[:], cstf_d[:, 0:T], writes=[cos_t[:]])
        dma('sp', sin_t[:], cstf_d[:, T:2 * T], writes=[sin_t[:]])
        wsv = w_next(l, 3)
        wvv = wsv[:].rearrange("p (k n) -> p k n", k=8)
        for tt_ in range(16):
            b = gbank()
            for kc in range(8):
                mm(b[:], xhi[:, kc, tt_ * 128:(tt_ + 1) * 128], wvv[:, kc, :], kc == 0, kc == 7)
            if tt_ % 2 == 0:
                act(vall[:, tt_, :], b[:], AF.Copy)
            else:
                cpy(vall[:, tt_, :], b[:])
        if stop == 'B0':
            return finish([fl(vall)])
        nb_rr = 0
        pt_rr = [0]
        memset(kpad[0][64:128, :], 0.0)
        memset(kpad[1][0:64, :], 0.0)
        for i_ in range(4):
            memset(biasT[i_][:, :], 0.0)
        for c in range(4):
            if stop is not None and stop.startswith('B3') and c == int(stop[2:]):
                return finish([fl(ya), fl(qbuf[0]), fl(kpad[0]), fl(vall)])
            wsp = w_next(l, 4 + c)
            qb_ = qbuf[0]
            kb8 = kbar[c % 2]
            kd = kdiff[c % 2]

            def rows_(hh):
                return slice(hh * 64, (hh + 1) * 64)

            def QS_(qb):
                return slice(qb * 256, (qb + 1) * 256)

            def emit_rope(tcs):
                for tc in tcs:
                    for which in (0, 1):
                        bz = gbank()
                        bs = gbank()
                        inproj(bz, wsp, which * 2, tc)
                        inproj(bs, wsp, which * 2 + 1, tc)
                        t1 = rt1[tc % 2][:]
                        t2 = rt2[tc % 2][:]
                        stt(t1, bz[:], bcol(4 + c, which * 2), cos_t[:, TC(tc)], ALU.add, ALU.mult)
                        stt(t2, bs[:], bcol(4 + c, which * 2 + 1), sin_t[:, TC(tc)], ALU.add, ALU.mult)
                        if which == 0:
                            tt(qb_[:, TC(tc)], t1, t2, ALU.add)
                        else:
                            tt(kpad[0][0:64, TC(tc)], t1[0:64, :], t2[0:64, :], ALU.add)
                            tt(kpad[1][64:128, TC(tc)], t1[64:128, :], t2[64:128, :], ALU.add)

            def emit_kdiff():
                for hh_ in range(2):
                    pr_ = slice(hh_ * 64, (hh_ + 1) * 64)
                    src_ = kpad[hh_][pr_, :]
                    dst_ = kb8[pr_, :]
                    S.op('dve', lambda h, src_=src_, dst_=dst_: h.tensor_reduce(
                        out=dst_, in_=src_.rearrange("p (n k) -> p n k", n=8), axis=AX.X, op=ALU.add),
                        reads=[src_], writes=[dst_])
                kdv = kd[:].rearrange("p (n m) -> p n m", n=8)
                for n in range(8):
                    ts(kdv[:, n, :], kb8[:, :], kb8[:, n:n + 1], None, ALU.subtract)

            bts = {}
            for qc_ in (2, 3):
                for hh_ in range(2):
                    bts[(qc_, hh_)] = biasT[(qc_ - 2) * 2 + hh_][:, :]
            grp_all = [(qb, hh) for qb in range(4, 8) for hh in range(2)]
            bks = {}

            def emit_bias1(g0):
                grp = grp_all[g0:g0 + 4]
                for key in grp:
                    qb, hh = key
                    bks[key] = gbank()
                    mm(bks[key][0:64, 0:256], kd[rows_(hh), :], qb_[rows_(hh), QS_(qb)], True, True)
                for i_, key in enumerate(grp):
                    ts(itile[i_][:, :], bks[key][0:64, 0:256], 0.0, None, ALU.is_gt)

            def emit_bias2(g0):
                grp = grp_all[g0:g0 + 4]
                for i_, key in enumerate(grp):
                    qb, hh = key
                    pr = slice(hh * 64, hh * 64 + 8)
                    tpk = {} if hh == 0 else {'tile_position': (0, 64)}
                    mm(bks[key][pr, 256:512], selc(qb - 4), itile[i_][:, :], True, True, **tpk)
                for i_, key in enumerate(grp):
                    qb, hh = key
                    pr = slice(hh * 64, hh * 64 + 8)
                    half = slice((qb % 2) * 256, (qb % 2 + 1) * 256)
                    ts(biasT[(qb // 2 - 2) * 2 + hh][pr, half], bks[key][pr, 256:512], 2.5, NEG,
                       ALU.is_ge, ALU.mult)

            steps = [(qc, hh, kt) for qc in range(4) for hh in range(2) for kt in range(4 * qc + 4)]
            DSKEW = 3
            pslot = {}
            accb = {}

            def phase1(st):
                qc, hh, kt = st
                rows = rows_(hh)
                n = kt // 2
                j = kt % 2
                bt = bts.get((qc, hh))
                use_mask = n >= 2 * qc
                use_bias = (bt is not None) and (n <= 2 * qc)
                bS = gbank()
                mm(bS[:], kpad[hh][:, kt * 128:(kt + 1) * 128], qb_[:, qc * 512:(qc + 1) * 512],
                   True, not (use_mask or use_bias))
                if use_mask:
                    mm(bS[:], ident, cm4((n - 2 * qc) * 2 + j), False, not use_bias)
                if use_bias:
                    mm(bS[:], ohf(n), bt, False, True)
                p_ = pT[pt_rr[0] % 4]
                pt_rr[0] += 1
                act(p_[:], bS[:], AF.Exp, scale=0.125)
                pslot[st] = p_

            def phase2(st):
                qc, hh, kt = st
                rows = rows_(hh)
                hcol = slice((2 * c + hh) * 64, (2 * c + hh + 1) * 64)
                tp = {} if hh == 0 else {'tile_position': (0, 64)}
                if qc not in accb:
                    accb[qc] = (lbank(), lbank())
                bo, bsum = accb[qc]
                p_ = pslot.pop(st)
                nkt = 4 * qc + 4
                first = (kt == 0)
                lastm = (kt == nkt - 1)
                mm(bo[rows, :], vall[:, kt, hcol], p_[:], first, lastm, **tp)
                mm(bsum[rows, :], ones1, p_[:], first, lastm, **tp)
                if hh == 1 and lastm:
                    rc = rcb[0][:]
                    nt = ntmp[0][:]
                    S.op('dve', lambda h, rc=rc, bsum=bsum: h.reciprocal(out=rc, in_=bsum[:]),
                         reads=[bsum[:]], writes=[rc])
                    tt(nt, bo[:], rc, ALU.mult)
                    act(ya[:, c, qc * 512:(qc + 1) * 512], nt, AF.Identity, bias=bcol(3, c))

            f1 = next(i for i, st in enumerate(steps) if st[0] == 1)
            events = {f1: [lambda: emit_rope([1, 2, 3]), emit_kdiff],
                      f1 + 9: [lambda: emit_bias1(0)], f1 + 14: [lambda: emit_bias2(0)],
                      f1 + 22: [lambda: emit_bias1(4)], f1 + 27: [lambda: emit_bias2(4)]}
            emit_rope([0])
            for i_ in range(len(steps) + DSKEW):
                for ev in events.get(i_, ()):
                    ev()
                if i_ < len(steps):
                    phase1(steps[i_])
                if i_ - DSKEW >= 0:
                    phase2(steps[i_ - DSKEW])

        if stop == 'B':
            return finish([fl(ya), fl(qbuf[0]), fl(kpad[1]), fl(vall)])
        for m in range(8):
            wsg = w_next(l, 8 + m)
            wpr = sm_next(l, 1 + m)
            wprv = wpr[:].rearrange("p (k n) -> p k n", k=10)
            ysrc = [(ya, 0, 4), (yb, 4, 2), (yc, 6, 2), (yd, 8, 2)]
            for tc in range(4):
                ac = macc[tc % 2][:]
                for i in range(4):
                    bgate = gbank()
                    bprj = gbank()
                    inproj(bgate, wsg, i, tc)
                    ysb, k0, nk = ysrc[i]
                    for k in range(nk):
                        mm(bprj[:], wprv[:, k0 + k, :], ysb[:, k, TC(tc)], k == 0, k == nk - 1)
                    sg = msg[i % 3][:]
                    act(sg, bgate[:], AF.Sigmoid, bias=bcol(8 + m, i))
                    if i == 0:
                        stt(ac, sg, INV_ALPHA, bprj[:], ALU.mult, ALU.mult)
                    else:
                        tm = mtm[i % 2][:]
                        stt(tm, sg, INV_ALPHA, bprj[:], ALU.mult, ALU.mult)
                        if i < 3:
                            tt(ac, ac, tm, ALU.add)
                        else:
                            tt(mix[:, m, TC(tc)], ac, tm, ALU.add)

        if stop == 'C':
            return finish([fl(mix)])
        wo0 = w_next(l, 16)
        wo1 = w_next(l, 17, keep=1)
        wov = [wo0[:].rearrange("p (k n) -> p k n", k=4), wo1[:].rearrange("p (k n) -> p k n", k=4)]
        pend = None
        for tc in range(4):
            rr = r32[tc % 2]
            for m in range(8):
                b = gbank()
                for k in range(8):
                    mm(b[:], wov[k // 4][:, k % 4, m * 128:(m + 1) * 128], mix[:, k, TC(tc)], k == 0, False)
                mm(b[:], ident, xhi[:, m, TC(tc)], False, False)
                mm(b[:], ident, xlo[:, m, TC(tc)], False, True)
                act(rr[:, m, :], b[:], AF.Copy)
                act(lsq[:, m, :], b[:], AF.Square)
                cpy(lrb[:, m, :], b[:])
            if pend is not None:
                ln_apply(*pend)
            sd, nmr = ln_stats(tc % 2, lrb, lsq)
            pend = (l, tc, [rr[:, m, :] for m in range(8)], sd, nmr, 140, 148, False)

        def mlp1_tile(w1, s_, j, tc):
            b = gbank()
            inproj(b, w1, j, tc)
            tm = htm[(j * 4 + tc) % 3][:]
            act(tm, b[:], AF.Relu, scale=INV_ALPHA)
            tt(hbuf[:, s_ * 4 + j, TC(tc)], tm, b[:], ALU.mult)
        w1_first = w_next(l, 18)
        for tc in (0, 1):
            for j in range(4):
                mlp1_tile(w1_first, 0, j, tc)
        ln_apply(*pend)

        if stop == 'D':
            return finish([fl(xhi), fl(xlo)])
        for g in range(4):
            for s in range(2):
                order = [(j, tc) for j in range(4) for tc in range(4)]
                if g == 0 and s == 0:
                    w1 = w1_first
                    order = [(j, tc) for tc in (2, 3) for j in range(4)]
                else:
                    w1 = w_next(l, 18 + 4 * g + s)
                    if g == 0:
                        order = [(j, tc) for tc in range(4) for j in range(4)]
                for j, tc in order:
                    mlp1_tile(w1, s, j, tc)
            for hf in range(2):
                w2 = w_next(l, 18 + 4 * g + 2 + hf)
                w2v = w2[:].rearrange("p (k n) -> p k n", k=8)
                for mq in range(4):
                    m = hf * 4 + mq
                    for tc in range(4):
                        b = gbank()
                        for fc in range(8):
                            mm(b[:], w2v[:, fc, mq * 128:(mq + 1) * 128], hbuf[:, fc, TC(tc)], fc == 0,
                               fc == 7 and g > 0)
                        if g == 0:
                            mm(b[:], ident, xhi[:, m, TC(tc)], False, False)
                            mm(b[:], ident, xlo[:, m, TC(tc)], False, True)
                            act(acc[:, m, TC(tc)], b[:], AF.Copy)
                        else:
                            tt(acc[:, m, TC(tc)], acc[:, m, TC(tc)], b[:], ALU.add)

        last = (l == L - 1)
        pend = None

        def fin(pend):
            ln_apply(*pend)
            if last:
                tcp = pend[1]
                o = dma('sp', outT_d.rearrange("(c p) t -> p c t", p=128)[:, :, TC(tcp)], acc[:, :, TC(tcp)],
                        reads=[acc[:, :, TC(tcp)]])
                out_dmas.append(o)
        for tc in range(4):
            for m in range(8):
                cpy(lrb2[:, m, :], acc[:, m, TC(tc)])
                act(lsq2[:, m, :], acc[:, m, TC(tc)], AF.Square)
            if pend is not None:
                fin(pend)
            sd, nmr = ln_stats(tc % 2, lrb2, lsq2)
            pend = (l, tc, [acc[:, m, TC(tc)] for m in range(8)], sd, nmr, 156, 164, last)
        fin(pend)

    S.emit(final_waits=out_dmas)
    return nc


def _rope_tables():
    inv = (np.float32(10000.0) ** (-np.arange(0, 64, 2, dtype=np.float32) / np.float32(64))).astype(np.float32)
    ang = (np.arange(SEQ, dtype=np.float32)[:, None] * inv[None, :]).astype(np.float32)
    cos = np.cos(ang).astype(np.float32).T
    sin = np.sin(ang).astype(np.float32).T
    r = np.arange(128)
    cosT = cos[r % 32]
    sgn = np.where((r % 64) < 32, -1.0, 1.0).astype(np.float32)[:, None]
    sinT = sin[r % 32] * sgn
    return cosT, sinT


def _constants():
    cosT, sinT = _rope_tables()
    cstf = np.zeros((128, NCF), np.float32)
    cstf[:, 0:SEQ] = cosT
    cstf[:, SEQ:2 * SEQ] = sinT
    wins = (2, 4, 8, 16)
    p = np.arange(128)
    for cc in range(2):
        w = np.array([wins[cc * 2 + (pp // 64)] for pp in p], np.float32)
        cstf[:, 4096 + cc] = 1.0 / w
        for t in range(16):
            cstf[:, 4098 + cc * 16 + t] = 1.0 / np.minimum(t + 1, w)
    cstf[:, 4130] = LN_EPS / (ALPHA * ALPHA)
    cstf[:, 4131] = LN_EPS
    cstb = np.zeros((128, NCB), np.float32)
    cstb[:, 0:128] = np.eye(128, dtype=np.float32)
    cstb[:, 128:256] = 1.0 / 1024.0
    cstb[:, 256:384] = 1.0 / 256.0
    cstb[:, 384:448] = 1.0
    key = np.arange(128)[:, None]
    q = np.arange(128)[None, :]
    cm = np.where(key <= q, 0.0, NEG).astype(np.float32)
    cstb[:, 448:576] = cm
    cstb[:, 576:704] = 0.0
    cstb[:, 704:832] = NEG
    cstb[:, 832:960] = cm
    for n in range(8):
        cstb[n, 960 + n * 128: 960 + (n + 1) * 128] = 1.0
        cstb[64 + n, 960 + n * 128: 960 + (n + 1) * 128] = 1.0
    for qi in range(4):
        qb = 4 + qi
        for n in range(8):
            for m in range(8):
                if m < qb and n < qb:
                    cstb[n * 8 + m, 1984 + qi * 8 + n] = 1.0
    for i in range(4):
        base = 2016 + i * 512
        for qq in range(4):
            blk = cstb[:, base + qq * 128: base + (qq + 1) * 128]
            if qq < i:
                blk[:] = NEG
            elif qq == i:
                blk[:] = cm
            else:
                blk[:] = 0.0
    return cstf, cstb


def _prep_weights(inp, L):
    f = np.float32
    swap = np.arange(512).reshape(8, 2, 32)[:, ::-1, :].reshape(512)
    wbig = np.zeros((L, NSLOT, 128, 4096), f)
    wsm = np.zeros((L, NSM, 128, 1280), f)
    par = np.zeros((128, L, NPAR), f)

    def slot_k8(w):
        return w.reshape(8, 128, 512).transpose(1, 0, 2).reshape(128, 4096)

    for l in range(L):
        w_in = inp['w_in'][l]
        b_in = inp['b_in'][l]
        cols = []
        cols.append(np.concatenate([np.arange(1536, 1792), np.arange(1792, 2048)]))
        cols.append(np.concatenate([np.arange(2048, 2304), np.arange(2304, 2560)]))
        cols.append(np.concatenate([np.arange(2816, 3072), np.arange(2560, 2816)]))
        cols.append(np.arange(1024, 1536))
        for c in range(4):
            qc = np.arange(c * 128, (c + 1) * 128)
            cols.append(np.concatenate([qc, swap[qc], 512 + qc, 512 + swap[qc]]))
        for m in range(8):
            cols.append(np.concatenate([3072 + i * 1024 + np.arange(m * 128, (m + 1) * 128) for i in range(4)]))
        for s, cs in enumerate(cols):
            wbig[l, s] = slot_k8(w_in[:, cs])
            par[:, l, s * 4:(s + 1) * 4] = b_in[cs].reshape(4, 128).T
        wo = inp['w_o'][l]
        for hk in range(2):
            wbig[l, 16 + hk] = wo[hk * 512:(hk + 1) * 512].reshape(4, 128, 1024).transpose(1, 0, 2).reshape(128, 4096)
        w1 = inp['w_mlp1'][l]
        w2 = inp['w_mlp2'][l]
        for g in range(4):
            for s in range(2):
                c0 = (2 * g + s) * 512
                wbig[l, 18 + 4 * g + s] = slot_k8(w1[:, c0:c0 + 512])
            for hf in range(2):
                blk = w2[g * 1024:(g + 1) * 1024, hf * 512:(hf + 1) * 512]
                wbig[l, 18 + 4 * g + 2 + hf] = slot_k8(blk)
        wp = inp['w_pool'][l]
        bd = np.zeros((2, 128, 128), f)
        for g in range(4):
            cc, hh = g // 2, g % 2
            bd[cc, hh * 64:(hh + 1) * 64, hh * 64:(hh + 1) * 64] = wp[g]
        wsm[l, 0, :, 0:256] = bd.transpose(1, 0, 2).reshape(128, 256)
        wpall = np.concatenate([inp['w_pa'][l], inp['w_pb'][l], inp['w_pc'][l], inp['w_pd'][l]], 0)
        for m in range(8):
            wsm[l, 1 + m] = wpall[:, m * 128:(m + 1) * 128].reshape(10, 128, 128).transpose(1, 0, 2).reshape(128, 1280)
        wc = inp['w_dw_conf'][l]
        for cc in range(2):
            par[:, l, 64 + cc * 31: 64 + (cc + 1) * 31] = wc[:, cc * 128:(cc + 1) * 128].T
            par[:, l, 126 + cc] = inp['b_dw_conf'][l][cc * 128:(cc + 1) * 128]
            par[:, l, 128 + cc] = inp['ln_conf_g'][l][cc * 128:(cc + 1) * 128]
            par[:, l, 130 + cc] = inp['ln_conf_b'][l][cc * 128:(cc + 1) * 128]
            par[:, l, 132 + cc] = inp['pool_scale'][l][cc * 128:(cc + 1) * 128]
            par[:, l, 134 + cc * 3: 134 + cc * 3 + 3] = inp['w_sc'][l][:, cc * 128:(cc + 1) * 128].T
        par[:, l, 140:148] = inp['ln1_g'][l].reshape(8, 128).T
        par[:, l, 148:156] = inp['ln1_b'][l].reshape(8, 128).T
        par[:, l, 156:164] = inp['ln2_g'][l].reshape(8, 128).T
        par[:, l, 164:172] = inp['ln2_b'][l].reshape(8, 128).T
    return wbig, wsm, np.ascontiguousarray(par.reshape(128, L * NPAR))


_CACHE = {}


def run(inputs, n_layers=DEPTH, cores=NCORES, debug=False):
    inp = {k: np.asarray(v, dtype=np.float32) for k, v in inputs.items()}
    key = (n_layers, debug)
    nc = build_program(n_layers, debug)
    wbig, wsm, par = _prep_weights(inp, n_layers)
    cstf, cstb = _constants()
    x = inp['x']
    in_maps = []
    for c in range(cores):
        in_maps.append({"xT": np.ascontiguousarray(x[c].T), "wbig": wbig, "wsm": wsm, "par": par,
                        "cstf": cstf, "cstb": cstb})
    res = run_bass_kernel_spmd(nc, in_maps, core_ids=list(range(cores)))
    outs = [np.ascontiguousarray(r["outT"].T) for r in res.results]
    return np.stack(outs, 0), res


def kernel(**inputs):
    out, _ = run(inputs)
    return out.astype(np.float32)
```

```python
import numpy as np
from contextlib import ExitStack
import concourse.bass as bass
import concourse.mybir as mybir
from concourse.bass_utils import run_bass_kernel_spmd

F32 = mybir.dt.float32
BF16 = mybir.dt.bfloat16
AF = mybir.ActivationFunctionType
ALU = mybir.AluOpType
AX = mybir.AxisListType

D_MODEL = 1024
SEQ = 2048
DEPTH = 4
NCORES = 8
ALPHA = (2.0 * DEPTH) ** 0.25
INV_ALPHA = 1.0 / ALPHA
LN_EPS = 1e-5
NEG = -30000.0

PAGE = 256
ENGS = ['pe', 'act', 'dve', 'pool', 'sp']
NDSEM = 8
MAXEPOCH = 8

NSLOT = 34
NSM = 9
NPAR = 176
NCF = 4136
NCB = 4064


class Op:
    __slots__ = ('eng', 'fn', 'deps', 'inc', 'epoch', 'dma', 'tok', 'idx', 'prewait')

    def __init__(self, eng, fn, epoch, dma):
        self.eng = eng
        self.fn = fn
        self.deps = []
        self.inc = False
        self.epoch = epoch
        self.dma = dma
        self.tok = None
        self.prewait = None


class Sched:
    def __init__(self, nc):
        self.nc = nc
        self.ops = {e: [] for e in ENGS}
        self.last_w = {}
        self.readers = {}
        self.epoch = 0
        self.tbase = {}
        self.dma_uses = {}
        self.dma_rr = {e: 0 for e in ENGS}
        self._pcache = {}

    def sb(self, name, shape, dtype, offset):
        t = self.nc.alloc_sbuf_tensor_at(name, list(shape), dtype, offset=offset)
        self.tbase[t.name] = ('sb', offset)
        return t

    def reg_psum(self, t, bank):
        self.tbase[t.name] = ('ps', bank * 2048)

    def keys(self, a):
        t = a.tensor
        ck = (t.name, a.offset, a.ap, a.dtype)
        r = self._pcache.get(ck)
        if r is not None:
            return r
        space, base = self.tbase[t.name]
        isz = mybir.dt.size(a.dtype)
        ap = a.ap
        pstep, pcnt = ap[0]
        if pstep == 0:
            p0 = 0
            foff = a.offset
        else:
            p0 = a.offset // pstep
            foff = a.offset % pstep
        q0, q1 = p0 // 32, (p0 + pcnt - 1) // 32
        dims = [d for d in ap[1:] if d[1] > 1 and d[0] != 0]
        if not dims:
            dims = [(1, 1)]
        inner = dims[-1]
        outer = dims[:-1]
        runlen = (inner[1] - 1) * abs(inner[0]) + 1
        pages = set()
        idx = [0] * len(outer)
        while True:
            st = foff + sum(i * d[0] for i, d in zip(idx, outer))
            b0 = base + st * isz
            b1 = base + (st + runlen) * isz - 1
            for pg in range(b0 // PAGE, b1 // PAGE + 1):
                pages.add(pg)
            k = len(outer) - 1
            while k >= 0:
                idx[k] += 1
                if idx[k] < outer[k][1]:
                    break
                idx[k] = 0
                k -= 1
            if k < 0:
                break
        if space == 'ps':
            banks = set(pg * PAGE // 2048 for pg in pages)
            r = [(space, b, q) for b in banks for q in range(q0, q1 + 1)]
        else:
            r = [(space, pg, q) for pg in pages for q in range(q0, q1 + 1)]
        self._pcache[ck] = r
        return r

    def op(self, eng, fn, reads=(), writes=(), dma=False):
        o = Op(eng, fn, self.epoch, dma)
        deps = set()
        ps_reads = [a for a in reads if self.tbase[a.tensor.name][0] == 'ps']
        if ps_reads:
            reads = [a for a in reads if self.tbase[a.tensor.name][0] != 'ps']
            writes = list(writes) + ps_reads
        for a in reads:
            for k in self.keys(a):
                w = self.last_w.get(k)
                if w is not None:
                    deps.add(w)
                self.readers.setdefault(k, []).append(o)
        for a in writes:
            for k in self.keys(a):
                w = self.last_w.get(k)
                if w is not None:
                    deps.add(w)
                rs = self.readers.get(k)
                if rs:
                    deps.update(rs)
                self.last_w[k] = o
                self.readers[k] = []
        deps.discard(o)
        o.deps = list(deps)
        for d in o.deps:
            if not d.dma and not (d.eng == 'pe' and eng == 'pe'):
                d.inc = True
        o.idx = len(self.ops[eng])
        self.ops[eng].append(o)
        if dma:
            k = self.dma_rr[eng] % NDSEM
            self.dma_rr[eng] += 1
            u = self.dma_uses.get((eng, k), 0)
            o.prewait = ((eng, k), 16 * u)
            self.dma_uses[(eng, k)] = u + 1
            o.tok = (('d', eng, k), 16 * (u + 1))
        return o

    def new_epoch(self):
        self.epoch += 1
        assert self.epoch < MAXEPOCH

    def emit(self, final_waits=()):
        nc = self.nc
        for e in ENGS:
            cnt = {}
            for o in self.ops[e]:
                if o.dma:
                    continue
                if o.inc:
                    c = cnt.get(o.epoch, 0) + 1
                    cnt[o.epoch] = c
                    o.tok = (('e', e, o.epoch), c)
        with ExitStack() as es:
            sems = {}
            for e in ENGS:
                for ep in range(self.epoch + 1):
                    sems[('e', e, ep)] = es.enter_context(nc.semaphore(f"s_{e}_{ep}"))
            for (e, k) in self.dma_uses:
                sems[('d', e, k)] = es.enter_context(nc.semaphore(f"d_{e}_{k}"))
            block = es.enter_context(nc.Block())
            handles = {'pe': block.tensor, 'act': block.scalar, 'dve': block.vector,
                       'pool': block.gpsimd, 'sp': block.sync}

            def make(e):
                def body(h):
                    seen = {}
                    for o in self.ops[e]:
                        toks = []
                        for d in o.deps:
                            if d.eng == e and e == 'pe' and not d.dma:
                                continue
                            toks.append(d.tok)
                        if o.dma:
                            (qe, k), v = o.prewait
                            if v > 0:
                                toks.append((('d', qe, k), v))
                        best = {}
                        for (s, v) in toks:
                            if v > best.get(s, 0):
                                best[s] = v
                        for s, v in best.items():
                            if seen.get(s, 0) >= v:
                                continue
                            h.wait_ge(sems[s], v)
                            seen[s] = v
                        ins = o.fn(h)
                        if o.dma:
                            ins.then_inc(sems[o.tok[0]], 16)
                        elif o.inc:
                            ins.then_inc(sems[o.tok[0]], 1)
                    if e == 'sp':
                        for o in final_waits:
                            s, v = o.tok
                            if seen.get(s, 0) < v:
                                h.wait_ge(sems[s], v)
                                seen[s] = v
                return body
            for e in ENGS:
                handles[e](make(e))


def build_program(n_layers=DEPTH, debug=False, stop=None):
    nc = bass.Bass("TRN2", target_bir_lowering=False)
    S = Sched(nc)
    L = n_layers
    T = SEQ

    xT_d = nc.dram_tensor("xT", [1024, T], F32, kind="ExternalInput").ap()
    wbig_d = nc.dram_tensor("wbig", [L, NSLOT, 128, 4096], F32, kind="ExternalInput").ap()
    wsm_d = nc.dram_tensor("wsm", [L, NSM, 128, 1280], F32, kind="ExternalInput").ap()
    par_d = nc.dram_tensor("par", [128, L * NPAR], F32, kind="ExternalInput").ap()
    cstf_d = nc.dram_tensor("cstf", [128, NCF], F32, kind="ExternalInput").ap()
    cstb_d = nc.dram_tensor("cstb", [128, NCB], F32, kind="ExternalInput").ap()
    outT_d = nc.dram_tensor("outT", [1024, T], F32, kind="ExternalOutput").ap()
    dbg_d = None
    if debug:
        dbg_d = nc.dram_tensor("dbg", [8, 128, 8 * T], F32, kind="ExternalOutput").ap()

    BASE = 16640
    LIMIT = 229344
    cur = [BASE]
    names = [0]

    def alloc(shape, dt, at=None):
        n = int(np.prod(shape[1:])) * mybir.dt.size(dt)
        n = (n + 63) // 64 * 64
        if at is None:
            off = (cur[0] + PAGE - 1) // PAGE * PAGE
            cur[0] = off + n
        else:
            off = at
        assert off + n <= LIMIT, (off, n)
        names[0] += 1
        return S.sb(f"t{names[0]}", shape, dt, off)

    xhi = alloc([128, 8, T], BF16)
    xlo = alloc([128, 8, T], BF16)
    par = alloc([128, L * NPAR], F32)
    csm = alloc([128, 40], F32)
    cb = alloc([128, NCB], BF16)
    wslot = [alloc([128, 4096], BF16) for _ in range(3)]
    wsml = [alloc([128, 1280], BF16) for _ in range(2)]
    D0 = (cur[0] + PAGE - 1) // PAGE * PAGE
    DYN = LIMIT - D0
    assert DYN >= 104900, DYN

    def at(off, shape, dt):
        return alloc(shape, dt, at=D0 + off)

    ident = cb[:, 0:128]
    onesD = cb[:, 128:256]
    ones256 = cb[:, 256:384]
    ones1 = cb[:, 384:448]
    cmA = cb[:, 448:704]
    cmB = cb[:, 704:960]

    def oh(n, hh=0):
        return cb[hh * 64: hh * 64 + 8, 960 + n * 128: 960 + (n + 1) * 128]

    def cm4(i):
        return cb[:, 2016 + i * 512: 2016 + (i + 1) * 512]

    def ohf(n):
        return cb[:, 960 + n * 128: 960 + (n + 1) * 128]

    def selc(q):
        return cb[0:64, 1984 + q * 8: 1984 + (q + 1) * 8]
    invw = csm[:, 0:2]
    f16 = csm[:, 2:34]
    eps_main = csm[:, 34:35]
    eps_conf = csm[:, 35:36]

    ya = at(0, [128, 4, T], BF16)
    yb = at(16384, [128, 2, T], BF16)
    yc = at(24576, [128, 2, T], BF16)
    yd = at(32768, [128, 2, T], BF16)
    cos_t = at(40960, [128, T], F32)
    sin_t = at(49152, [128, T], F32)
    ATT = 57344
    qbuf = [at(ATT, [128, T], BF16)]
    kpad = [at(ATT + 4096 + i * 4096, [128, T], BF16) for i in range(2)]
    vall = at(ATT + 16384, [128, 16, 512], BF16)
    A2 = ATT + 32768
    pT = [at(A2 + i * 1024, [128, 512], BF16) for i in range(4)]
    rcb = [at(A2 + 4096, [128, 512], F32)]
    ntmp = [at(A2 + 6144, [128, 512], F32)]
    rt1 = [at(ATT + 12288, [128, 512], F32)] * 2
    rt2 = [at(ATT + 14336, [128, 512], F32)] * 2
    biasT = [at(A2 + 8192 + i * 1024, [128, 512], BF16) for i in range(4)]
    itile = [at(A2 + 12288 + i * 512, [64, 256], BF16) for i in range(4)]
    kbar = [at(A2 + 14336 + i * 64, [128, 8], F32) for i in range(2)]
    kdiff = [at(A2 + 14464 + i * 128, [128, 64], BF16) for i in range(2)]
    assert A2 + 14720 <= 104900
    SA = 40960
    cub = at(65536, [128, 2, 30 + T], BF16)
    crb = at(73856, [128, 2, 512], BF16)
    csq = at(75904, [128, 2, 512], BF16)
    csg = [at(82048 + i * 2048, [128, 512], F32) for i in range(2)]
    dg = at(86144, [128, 2, 31, 128], BF16)
    cst = [at(77952, [128, 512], F32), at(80000, [128, 512], F32), at(102016, [128, 512], F32)]
    pup = at(SA, [128, 32 + T], F32)
    psA = at(SA + 8320, [128, 32 + T], F32)
    psB = at(SA + 16640, [128, 32 + T], F32)
    pd = at(SA + 24960, [128, T], BF16)
    pd2 = at(SA + 8320, [128, T], BF16)
    pt16 = at(SA + 29056, [128, 16], F32)
    scx = at(SA + 29184, [128, 2, T], F32)
    scp = at(SA + 45568, [128, 2, 2 + T], F32)
    assert SA + 62016 <= 104900
    mix = at(40960, [128, 8, T], BF16)
    msg = [at(73728 + i * 2048, [128, 512], F32) for i in range(3)]
    mtm = [at(79872 + i * 2048, [128, 512], F32) for i in range(2)]
    macc = [at(83968 + i * 2048, [128, 512], F32) for i in range(2)]
    r32 = [at(i * 16384, [128, 8, 512], F32) for i in range(2)]
    lrb = at(73728, [128, 8, 512], BF16)
    lsq = at(81920, [128, 8, 512], BF16)
    lst = [[at(98304 + i * 2048, [128, 512], F32) for i in range(3)],
           [at(90112 + i * 2048, [128, 512], F32) for i in range(3)]]
    acc = at(0, [128, 8, T], F32)
    hbuf = at(65536, [128, 8, T], BF16)
    htm = [at(98304 + i * 2048, [128, 512], F32) for i in range(3)]
    lrb2 = at(65536, [128, 8, 512], BF16)
    lsq2 = at(73728, [128, 8, 512], BF16)

    ps = []
    for i in range(8):
        t = nc.alloc_psum_tensor(f"ps{i}", [128, 512], F32)
        S.reg_psum(t, i)
        ps.append(t)
    g_rr = [0]
    l_rr = [0]

    def gbank():
        b = ps[g_rr[0] % 4]
        g_rr[0] += 1
        return b

    def lbank():
        b = ps[4 + l_rr[0] % 4]
        l_rr[0] += 1
        return b

    def isap(v):
        return not isinstance(v, (int, float)) and v is not None

    def mm(out, lhsT, rhs, start, stop, **kw):
        S.op('pe', lambda h: h.matmul(out, lhsT=lhsT, rhs=rhs, start=start, stop=stop, **kw),
             reads=[lhsT, rhs], writes=[out])

    def act(out, in_, func, bias=None, scale=None, eng='act'):
        rd = [in_]
        kw = {}
        if bias is not None:
            kw['bias'] = bias
            if isap(bias):
                rd.append(bias)
        if scale is not None:
            kw['scale'] = scale
            if isap(scale):
                rd.append(scale)
        S.op('act', lambda h: h.activation(out=out, in_=in_, func=func, **kw), reads=rd, writes=[out])

    def tt(out, in0, in1, op, eng='dve'):
        S.op(eng, lambda h: h.tensor_tensor(out=out, in0=in0, in1=in1, op=op), reads=[in0, in1], writes=[out])

    def ts(out, in0, s1, s2, op0, op1=None, eng='dve'):
        rd = [in0] + [s for s in (s1, s2) if isap(s)]
        if op1 is None:
            S.op(eng, lambda h: h.tensor_scalar(out=out, in0=in0, scalar1=s1, scalar2=None, op0=op0),
                 reads=rd, writes=[out])
        else:
            S.op(eng, lambda h: h.tensor_scalar(out=out, in0=in0, scalar1=s1, scalar2=s2, op0=op0, op1=op1),
                 reads=rd, writes=[out])

    def stt(out, in0, scalar, in1, op0, op1, eng='dve'):
        rd = [in0, in1] + ([scalar] if isap(scalar) else [])
        S.op(eng, lambda h: h.scalar_tensor_tensor(out=out, in0=in0, scalar=scalar, in1=in1, op0=op0, op1=op1),
             reads=rd, writes=[out])

    def cpy(out, in_, eng='dve'):
        S.op(eng, lambda h: h.tensor_copy(out=out, in_=in_), reads=[in_], writes=[out])

    def dma(q, out, in_, reads=(), writes=()):
        return S.op(q, lambda h: h.dma_start(out=out, in_=in_), reads=reads, writes=writes, dma=True)

    def memset(ap, val, eng='dve'):
        S.op(eng, lambda h: h.memset(ap, val), writes=[ap])

    wq = []
    for l in range(L):
        for s in range(NSLOT):
            wq.append((l, s))
    wstate = {'issued': 0, 'cons': 0}

    def w_issue():
        i = wstate['issued']
        if i >= len(wq):
            return
        l, s = wq[i]
        buf = wslot[i % 3]
        dma('pool', buf[:], wbig_d[l, s], writes=[buf[:]])
        wstate['issued'] += 1

    def w_next(l, s, keep=0):
        i = wstate['cons']
        assert wq[i] == (l, s), (wq[i], l, s)
        wstate['cons'] += 1
        while wstate['issued'] < min(i - keep + 3, len(wq)):
            w_issue()
        return wslot[i % 3]

    smq = [(l, s) for l in range(L) for s in range(NSM)]
    smstate = {'issued': 0, 'cons': 0}

    def sm_issue():
        i = smstate['issued']
        if i >= len(smq):
            return
        l, s = smq[i]
        buf = wsml[i % 2]
        dma('pool', buf[:], wsm_d[l, s], writes=[buf[:]])
        smstate['issued'] += 1

    def sm_next(l, s):
        i = smstate['cons']
        assert smq[i] == (l, s)
        smstate['cons'] += 1
        while smstate['issued'] < min(i + 2, len(smq)):
            sm_issue()
        return wsml[i % 2]

    dma('sp', par[:], par_d, writes=[par[:]])
    dma('sp', csm[:, 0:36], cstf_d[:, 4096:4132], writes=[csm[:, 0:36]])
    dma('pool', cb[:], cstb_d, writes=[cb[:]])
    xin = acc
    dma('sp', xin[:], xT_d.rearrange("(c p) t -> p c t", p=128), writes=[xin[:]])
    for c in range(8):
        act(xhi[:, c, :], xin[:, c, :], AF.Copy)
        tt(xlo[:, c, :], xin[:, c, :], xhi[:, c, :], ALU.subtract)
    w_issue()
    w_issue()
    sm_issue()

    def TC(tc):
        return slice(tc * 512, (tc + 1) * 512)

    out_dmas = []

    def ln_stats(pset, rb, sq):
        b1 = lbank()
        b2 = lbank()
        for c in range(8):
            mm(b1[:], onesD, rb[:, c, :], c == 0, c == 7)
        for c in range(8):
            mm(b2[:], onesD, sq[:, c, :], c == 0, c == 7)
        m2, sd, nmr = lst[pset][0][:], lst[pset][1][:], lst[pset][2][:]
        act(m2, b1[:], AF.Square)
        tt(m2, b2[:], m2, ALU.subtract)
        act(sd, m2, AF.Sqrt, bias=eps_main)
        S.op('dve', lambda h: h.reciprocal(out=sd, in_=sd), reads=[sd], writes=[sd])
        stt(nmr, b1[:], -1.0, sd, ALU.mult, ALU.mult)
        return sd, nmr

    def ln_apply(l, tc, rin, sd, nmr, gcol, bcol, last):
        for c in range(8):
            r = rin[c]
            tt(r, r, sd, ALU.mult)
            tt(r, r, nmr, ALU.add)
        for c in range(8):
            r = rin[c]
            g = par[:, l * NPAR + gcol + c: l * NPAR + gcol + c + 1]
            b = par[:, l * NPAR + bcol + c: l * NPAR + bcol + c + 1]
            act(r, r, AF.Identity, bias=b, scale=g)
        if not last:
            for c in range(8):
                act(xhi[:, c, TC(tc)], rin[c], AF.Copy)
            for c in range(8):
                tt(xlo[:, c, TC(tc)], rin[c], xhi[:, c, TC(tc)], ALU.subtract, eng='pool')

    def finish(dumps):
        outs = []
        for i, a in enumerate(dumps):
            n = a.shape[1]
            outs.append(dma('pool', dbg_d[i, 0:a.shape[0], 0:n], a, reads=[a]))
        S.emit(final_waits=outs)
        return nc

    def fl(t):
        a = t[:]
        if len(a.shape) == 3:
            a = a.rearrange("p a b -> p (a b)")
        return a

    for l in range(L):
        if l > 0:
            S.new_epoch()
        P = l * NPAR

        def bcol(slot, j):
            return par[:, P + slot * 4 + j: P + slot * 4 + j + 1]

        def pcol(i):
            return par[:, P + i: P + i + 1]

        def inproj(bank, ws, j, tc):
            wv = ws[:].rearrange("p (k n) -> p k n", k=8)
            for kc in range(8):
                mm(bank[:], wv[:, kc, j * 128:(j + 1) * 128], xhi[:, kc, TC(tc)], kc == 0, kc == 7)

        ws = w_next(l, 0)
        for cc in range(2):
            memset(cub[:, cc, 0:30], 0.0)
        for tc in range(4):
            for cc in range(2):
                ba = gbank()
                bg = gbank()
                inproj(ba, ws, cc, tc)
                inproj(bg, ws, 2 + cc, tc)
                sg = csg[cc][:]
                act(sg, bg[:], AF.Sigmoid, bias=bcol(0, 2 + cc))
                stt(cub[:, cc, 30 + tc * 512: 30 + (tc + 1) * 512], ba[:], bcol(0, cc), sg, ALU.add, ALU.mult)
        for cc in range(2):
            dgv = dg[:, cc, :, :]
            idb = bass.AP(ident.tensor, ident.offset, [list(ident.ap[0]), [0, 31], [1, 128]])
            wc_ = par[:, P + 64 + cc * 31: P + 64 + (cc + 1) * 31]
            wb = bass.AP(wc_.tensor, wc_.offset, [list(wc_.ap[0]), [1, 31], [0, 128]])
            S.op('dve', lambda h, dgv=dgv, idb=idb, wb=wb: h.tensor_tensor(out=dgv, in0=idb, in1=wb, op=ALU.mult),
                 reads=[ident, wc_], writes=[dgv])
        for tc in range(4):
            cbk = []
            for cc in range(2):
                bk = gbank()
                cbk.append(bk)
                for k in range(31):
                    mm(bk[:], dg[:, cc, k, :], cub[:, cc, tc * 512 + k: tc * 512 + k + 512], k == 0, k == 30)
            for cc in range(2):
                act(crb[:, cc, :], cbk[cc][:], AF.Identity, bias=pcol(126 + cc))
                act(csq[:, cc, :], cbk[cc][:], AF.Square, bias=pcol(126 + cc))
            b1 = lbank()
            b2 = lbank()
            for cc in range(2):
                mm(b1[:], ones256, crb[:, cc, :], cc == 0, cc == 1)
            for cc in range(2):
                mm(b2[:], ones256, csq[:, cc, :], cc == 0, cc == 1)
            m2, sd, nmr = cst[0][:], cst[1][:], cst[2][:]
            act(m2, b1[:], AF.Square)
            tt(m2, b2[:], m2, ALU.subtract)
            act(sd, m2, AF.Sqrt, bias=eps_conf)
            S.op('dve', lambda h, sd=sd: h.reciprocal(out=sd, in_=sd), reads=[sd], writes=[sd])
            stt(nmr, b1[:], -1.0, sd, ALU.mult, ALU.mult)
            for cc in range(2):
                tmp = csg[1][:] if cc == 0 else csg[0][:]
                stt(tmp, cbk[cc][:], pcol(126 + cc), sd, ALU.add, ALU.mult)
                tt(tmp, tmp, nmr, ALU.add)
                act(yb[:, cc, TC(tc)], tmp, AF.Silu, bias=pcol(130 + cc), scale=pcol(128 + cc))

        ws6 = w_next(l, 1)
        wpool = sm_next(l, 0)
        wpv = wpool[:, 0:256].rearrange("p (c n) -> p c n", c=2)
        for cc in range(2):
            memset(pup[:, 0:32], 0.0)
            for tc in range(4):
                b = gbank()
                inproj(b, ws6, cc, tc)
                act(pup[:, 32 + tc * 512: 32 + (tc + 1) * 512], b[:], AF.Identity, bias=bcol(1, cc))
            E = 32 + T
            tt(psA[:, 1:E], pup[:, 1:E], pup[:, 0:E - 1], ALU.add)
            tt(psB[:, 3:E], psA[:, 3:E], psA[:, 1:E - 2], ALU.add)
            if cc == 1:
                tt(psA[:, 7:E], psB[:, 7:E], psB[:, 3:E - 4], ALU.add)
                tt(psB[:, 15:E], psA[:, 15:E], psA[:, 7:E - 8], ALU.add)
            cpy(psB[0:64, 32:E], psA[0:64, 32:E])
            pdc = pd if cc == 0 else pd2
            stt(pdc[:, :], psB[:, 32:E], invw[:, cc:cc + 1], pup[:, 32:E], ALU.mult, ALU.subtract)
            tt(pt16[:, :], psB[:, 32:48], f16[:, cc * 16:(cc + 1) * 16], ALU.mult)
            tt(pdc[:, 0:16], pt16[:, :], pup[:, 32:48], ALU.subtract)
        for cc in range(2):
            for tc in range(4):
                b = gbank()
                inproj(b, ws6, 2 + cc, tc)
                act(scx[:, cc, TC(tc)], b[:], AF.Identity, bias=bcol(1, 2 + cc))
        for cc in range(2):
            pdc = pd if cc == 0 else pd2
            for tc in range(4):
                b = gbank()
                mm(b[:], wpv[:, cc, :], pdc[:, TC(tc)], True, True)
                act(yc[:, cc, TC(tc)], b[:], AF.Identity, scale=pcol(132 + cc))
        ws7 = w_next(l, 2)
        for cc in range(2):
            memset(scp[:, cc, 0:2], 0.0)
            for tc in range(4):
                b = gbank()
                inproj(b, ws7, cc, tc)
                stt(scp[:, cc, 2 + tc * 512: 2 + (tc + 1) * 512], b[:], bcol(2, cc), scx[:, cc, TC(tc)],
                    ALU.add, ALU.mult)
            cv = scx[:, cc, :]
            ts(cv, scp[:, cc, 0:T], pcol(134 + cc * 3 + 0), None, ALU.mult)
            stt(cv, scp[:, cc, 1:T + 1], pcol(134 + cc * 3 + 1), cv, ALU.mult, ALU.add)
            stt(cv, scp[:, cc, 2:T + 2], pcol(134 + cc * 3 + 2), cv, ALU.mult, ALU.add)
        for cc in range(2):
            for tc in range(4):
                b = gbank()
                inproj(b, ws7, 2 + cc, tc)
                stt(yd[:, cc, TC(tc)], b[:], bcol(2, 2 + cc), scx[:, cc, TC(tc)], ALU.add, ALU.mult)

        if stop == 'A':
            return finish([fl(yb), fl(yc), fl(yd)])
        dma('sp', cos_t[:], cstf_d[:, 0:T], writes=[cos_t[:]])
        dma('sp', sin_t[:], cstf_d[:, T:2 * T], writes=[sin_t[:]])
        wsv = w_next(l, 3)
        wvv = wsv[:].rearrange("p (k n) -> p k n", k=8)
        for tt_ in range(16):
            b = gbank()
            for kc in range(8):
                mm(b[:], xhi[:, kc, tt_ * 128:(tt_ + 1) * 128], wvv[:, kc, :], kc == 0, kc == 7)
            if tt_ % 2 == 0:
                act(vall[:, tt_, :], b[:], AF.Copy)
            else:
                cpy(vall[:, tt_, :], b[:])
        if stop == 'B0':
            return finish([fl(vall)])
        nb_rr = 0
        pt_rr = [0]
        memset(kpad[0][64:128, :], 0.0)
        memset(kpad[1][0:64, :], 0.0)
        for i_ in range(4):
            memset(biasT[i_][:, :], 0.0)
        for c in range(4):
            if stop is not None and stop.startswith('B3') and c == int(stop[2:]):
                return finish([fl(ya), fl(qbuf[0]), fl(kpad[0]), fl(vall)])
            wsp = w_next(l, 4 + c)
            qb_ = qbuf[0]
            kb8 = kbar[c % 2]
            kd = kdiff[c % 2]

            def rows_(hh):
                return slice(hh * 64, (hh + 1) * 64)

            def QS_(qb):
                return slice(qb * 256, (qb + 1) * 256)

            def emit_rope(tcs):
                for tc in tcs:
                    for which in (0, 1):
                        bz = gbank()
                        bs = gbank()
                        inproj(bz, wsp, which * 2, tc)
                        inproj(bs, wsp, which * 2 + 1, tc)
                        t1 = rt1[tc % 2][:]
                        t2 = rt2[tc % 2][:]
                        stt(t1, bz[:], bcol(4 + c, which * 2), cos_t[:, TC(tc)], ALU.add, ALU.mult)
                        stt(t2, bs[:], bcol(4 + c, which * 2 + 1), sin_t[:, TC(tc)], ALU.add, ALU.mult)
                        if which == 0:
                            tt(qb_[:, TC(tc)], t1, t2, ALU.add)
                        else:
                            tt(kpad[0][0:64, TC(tc)], t1[0:64, :], t2[0:64, :], ALU.add)
                            tt(kpad[1][64:128, TC(tc)], t1[64:128, :], t2[64:128, :], ALU.add)

            def emit_kdiff():
                for hh_ in range(2):
                    pr_ = slice(hh_ * 64, (hh_ + 1) * 64)
                    src_ = kpad[hh_][pr_, :]
                    dst_ = kb8[pr_, :]
                    S.op('dve', lambda h, src_=src_, dst_=dst_: h.tensor_reduce(
                        out=dst_, in_=src_.rearrange("p (n k) -> p n k", n=8), axis=AX.X, op=ALU.add),
                        reads=[src_], writes=[dst_])
                kdv = kd[:].rearrange("p (n m) -> p n m", n=8)
                kfull = kb8[:, :]
                in_m = bass.AP(kfull.tensor, kfull.offset, [list(kfull.ap[0]), [0, 8], [1, 8]])
                in_n = bass.AP(kfull.tensor, kfull.offset, [list(kfull.ap[0]), [1, 8], [0, 8]])
                S.op('dve', lambda h, kdv=kdv, in_m=in_m, in_n=in_n: h.tensor_tensor(
                    out=kdv, in0=in_m, in1=in_n, op=ALU.subtract), reads=[kfull], writes=[kdv])

            bts = {}
            for qc_ in (2, 3):
                for hh_ in range(2):
                    bts[(qc_, hh_)] = biasT[(qc_ - 2) * 2 + hh_][:, :]
            grp_all = [(qb, hh) for qb in range(4, 8) for hh in range(2)]
            bks = {}

            def emit_bias1(g0):
                grp = grp_all[g0:g0 + 4]
                for key in grp:
                    qb, hh = key
                    bks[key] = gbank()
                    mm(bks[key][0:64, 0:256], kd[rows_(hh), :], qb_[rows_(hh), QS_(qb)], True, True)
                for i_, key in enumerate(grp):
                    ts(itile[i_][:, :], bks[key][0:64, 0:256], 0.0, None, ALU.is_gt)

            def emit_bias2(g0):
                grp = grp_all[g0:g0 + 4]
                for i_, key in enumerate(grp):
                    qb, hh = key
                    pr = slice(hh * 64, hh * 64 + 8)
                    tpk = {} if hh == 0 else {'tile_position': (0, 64)}
                    mm(bks[key][pr, 256:512], selc(qb - 4), itile[i_][:, :], True, True, **tpk)
                for i_, key in enumerate(grp):
                    qb, hh = key
                    pr = slice(hh * 64, hh * 64 + 8)
                    half = slice((qb % 2) * 256, (qb % 2 + 1) * 256)
                    ts(biasT[(qb // 2 - 2) * 2 + hh][pr, half], bks[key][pr, 256:512], 2.5, NEG,
                       ALU.is_ge, ALU.mult)

            steps = [(qc, hh, kt) for qc in range(4) for hh in range(2) for kt in range(4 * qc + 4)]
            DSKEW = 3
            pslot = {}
            accb = {}

            def phase1(st):
                qc, hh, kt = st
                rows = rows_(hh)
                n = kt // 2
                j = kt % 2
                bt = bts.get((qc, hh))
                use_mask = n >= 2 * qc
                use_bias = (bt is not None) and (n <= 2 * qc)
                bS = gbank()
                mm(bS[:], kpad[hh][:, kt * 128:(kt + 1) * 128], qb_[:, qc * 512:(qc + 1) * 512],
                   True, not (use_mask or use_bias))
                if use_mask:
                    mm(bS[:], ident, cm4((n - 2 * qc) * 2 + j), False, not use_bias)
                if use_bias:
                    mm(bS[:], ohf(n), bt, False, True)
                p_ = pT[pt_rr[0] % 4]
                pt_rr[0] += 1
                act(p_[:], bS[:], AF.Exp, scale=0.125)
                pslot[st] = p_

            def phase2(st):
                qc, hh, kt = st
                rows = rows_(hh)
                hcol = slice((2 * c + hh) * 64, (2 * c + hh + 1) * 64)
                tp = {} if hh == 0 else {'tile_position': (0, 64)}
                if qc not in accb:
                    accb[qc] = (lbank(), lbank())
                bo, bsum = accb[qc]
                p_ = pslot.pop(st)
                nkt = 4 * qc + 4
                first = (kt == 0)
                lastm = (kt == nkt - 1)
                mm(bo[rows, :], vall[:, kt, hcol], p_[:], first, lastm, **tp)
                mm(bsum[rows, :], ones1, p_[:], first, lastm, **tp)
                if hh == 1 and lastm:
                    rc = rcb[0][:]
                    nt = ntmp[0][:]
                    S.op('dve', lambda h, rc=rc, bsum=bsum: h.reciprocal(out=rc, in_=bsum[:]),
                         reads=[bsum[:]], writes=[rc])
                    tt(nt, bo[:], rc, ALU.mult)
                    act(ya[:, c, qc * 512:(qc + 1) * 512], nt, AF.Identity, bias=bcol(3, c))

            f1 = next(i for i, st in enumerate(steps) if st[0] == 1)
            events = {f1: [lambda: emit_rope([1, 2, 3]), emit_kdiff],
                      f1 + 9: [lambda: emit_bias1(0)], f1 + 14: [lambda: emit_bias2(0)],
                      f1 + 22: [lambda: emit_bias1(4)], f1 + 27: [lambda: emit_bias2(4)]}
            emit_rope([0])
            for i_ in range(len(steps) + DSKEW):
                for ev in events.get(i_, ()):
                    ev()
                if i_ < len(steps):
                    phase1(steps[i_])
                if i_ - DSKEW >= 0:
                    phase2(steps[i_ - DSKEW])

        if stop == 'B':
            return finish([fl(ya), fl(qbuf[0]), fl(kpad[1]), fl(vall)])
        for m in range(8):
            wsg = w_next(l, 8 + m)
            wpr = sm_next(l, 1 + m)
            wprv = wpr[:].rearrange("p (k n) -> p k n", k=10)
            ysrc = [(ya, 0, 4), (yb, 4, 2), (yc, 6, 2), (yd, 8, 2)]
            for tc in range(4):
                ac = macc[tc % 2][:]
                for i in range(4):
                    bgate = gbank()
                    bprj = gbank()
                    inproj(bgate, wsg, i, tc)
                    ysb, k0, nk = ysrc[i]
                    for k in range(nk):
                        mm(bprj[:], wprv[:, k0 + k, :], ysb[:, k, TC(tc)], k == 0, k == nk - 1)
                    sg = msg[i % 3][:]
                    act(sg, bgate[:], AF.Sigmoid, bias=bcol(8 + m, i))
                    if i == 0:
                        stt(ac, sg, INV_ALPHA, bprj[:], ALU.mult, ALU.mult)
                    else:
                        tm = mtm[i % 2][:]
                        stt(tm, sg, INV_ALPHA, bprj[:], ALU.mult, ALU.mult)
                        if i < 3:
                            tt(ac, ac, tm, ALU.add)
                        else:
                            tt(mix[:, m, TC(tc)], ac, tm, ALU.add)

        if stop == 'C':
            return finish([fl(mix)])
        wo0 = w_next(l, 16)
        wo1 = w_next(l, 17, keep=1)
        wov = [wo0[:].rearrange("p (k n) -> p k n", k=4), wo1[:].rearrange("p (k n) -> p k n", k=4)]
        pend = None
        for tc in range(4):
            rr = r32[tc % 2]
            for m in range(8):
                b = gbank()
                for k in range(8):
                    mm(b[:], wov[k // 4][:, k % 4, m * 128:(m + 1) * 128], mix[:, k, TC(tc)], k == 0, False)
                mm(b[:], ident, xhi[:, m, TC(tc)], False, False)
                mm(b[:], ident, xlo[:, m, TC(tc)], False, True)
                act(rr[:, m, :], b[:], AF.Copy)
                act(lsq[:, m, :], b[:], AF.Square)
                cpy(lrb[:, m, :], b[:])
            if pend is not None:
                ln_apply(*pend)
            sd, nmr = ln_stats(tc % 2, lrb, lsq)
            pend = (l, tc, [rr[:, m, :] for m in range(8)], sd, nmr, 140, 148, False)

        def mlp1_tile(w1, s_, j, tc):
            b = gbank()
            inproj(b, w1, j, tc)
            tm = htm[(j * 4 + tc) % 3][:]
            act(tm, b[:], AF.Relu, scale=INV_ALPHA)
            tt(hbuf[:, s_ * 4 + j, TC(tc)], tm, b[:], ALU.mult)
        w1_first = w_next(l, 18)
        for tc in (0, 1):
            for j in range(4):
                mlp1_tile(w1_first, 0, j, tc)
        ln_apply(*pend)

        if stop == 'D':
            return finish([fl(xhi), fl(xlo)])
        for g in range(4):
            for s in range(2):
                order = [(j, tc) for j in range(4) for tc in range(4)]
                if g == 0 and s == 0:
                    w1 = w1_first
                    order = [(j, tc) for tc in (2, 3) for j in range(4)]
                else:
                    w1 = w_next(l, 18 + 4 * g + s)
                    if g == 0:
                        order = [(j, tc) for tc in range(4) for j in range(4)]
                for j, tc in order:
                    mlp1_tile(w1, s, j, tc)
            for hf in range(2):
                w2 = w_next(l, 18 + 4 * g + 2 + hf)
                w2v = w2[:].rearrange("p (k n) -> p k n", k=8)
                for mq in range(4):
                    m = hf * 4 + mq
                    for tc in range(4):
                        b = gbank()
                        for fc in range(8):
                            mm(b[:], w2v[:, fc, mq * 128:(mq + 1) * 128], hbuf[:, fc, TC(tc)], fc == 0,
                               fc == 7 and g > 0)
                        if g == 0:
                            mm(b[:], ident, xhi[:, m, TC(tc)], False, False)
                            mm(b[:], ident, xlo[:, m, TC(tc)], False, True)
                            act(acc[:, m, TC(tc)], b[:], AF.Copy)
                        else:
                            tt(acc[:, m, TC(tc)], acc[:, m, TC(tc)], b[:], ALU.add)

        last = (l == L - 1)
        pend = None

        def fin(pend):
            ln_apply(*pend)
            if last:
                tcp = pend[1]
                o = dma('sp', outT_d.rearrange("(c p) t -> p c t", p=128)[:, :, TC(tcp)], acc[:, :, TC(tcp)],
                        reads=[acc[:, :, TC(tcp)]])
                out_dmas.append(o)
        for tc in range(4):
            for m in range(8):
                cpy(lrb2[:, m, :], acc[:, m, TC(tc)])
                act(lsq2[:, m, :], acc[:, m, TC(tc)], AF.Square)
            if pend is not None:
                fin(pend)
            sd, nmr = ln_stats(tc % 2, lrb2, lsq2)
            pend = (l, tc, [acc[:, m, TC(tc)] for m in range(8)], sd, nmr, 156, 164, last)
        fin(pend)

    S.emit(final_waits=out_dmas)
    return nc


def _rope_tables():
    inv = (np.float32(10000.0) ** (-np.arange(0, 64, 2, dtype=np.float32) / np.float32(64))).astype(np.float32)
    ang = (np.arange(SEQ, dtype=np.float32)[:, None] * inv[None, :]).astype(np.float32)
    cos = np.cos(ang).astype(np.float32).T
    sin = np.sin(ang).astype(np.float32).T
    r = np.arange(128)
    cosT = cos[r % 32]
    sgn = np.where((r % 64) < 32, -1.0, 1.0).astype(np.float32)[:, None]
    sinT = sin[r % 32] * sgn
    return cosT, sinT


def _constants():
    cosT, sinT = _rope_tables()
    cstf = np.zeros((128, NCF), np.float32)
    cstf[:, 0:SEQ] = cosT
    cstf[:, SEQ:2 * SEQ] = sinT
    wins = (2, 4, 8, 16)
    p = np.arange(128)
    for cc in range(2):
        w = np.array([wins[cc * 2 + (pp // 64)] for pp in p], np.float32)
        cstf[:, 4096 + cc] = 1.0 / w
        for t in range(16):
            cstf[:, 4098 + cc * 16 + t] = 1.0 / np.minimum(t + 1, w)
    cstf[:, 4130] = LN_EPS / (ALPHA * ALPHA)
    cstf[:, 4131] = LN_EPS
    cstb = np.zeros((128, NCB), np.float32)
    cstb[:, 0:128] = np.eye(128, dtype=np.float32)
    cstb[:, 128:256] = 1.0 / 1024.0
    cstb[:, 256:384] = 1.0 / 256.0
    cstb[:, 384:448] = 1.0
    key = np.arange(128)[:, None]
    q = np.arange(128)[None, :]
    cm = np.where(key <= q, 0.0, NEG).astype(np.float32)
    cstb[:, 448:576] = cm
    cstb[:, 576:704] = 0.0
    cstb[:, 704:832] = NEG
    cstb[:, 832:960] = cm
    for n in range(8):
        cstb[n, 960 + n * 128: 960 + (n + 1) * 128] = 1.0
        cstb[64 + n, 960 + n * 128: 960 + (n + 1) * 128] = 1.0
    for qi in range(4):
        qb = 4 + qi
        for n in range(8):
            for m in range(8):
                if m < qb and n < qb:
                    cstb[n * 8 + m, 1984 + qi * 8 + n] = 1.0
    for i in range(4):
        base = 2016 + i * 512
        for qq in range(4):
            blk = cstb[:, base + qq * 128: base + (qq + 1) * 128]
            if qq < i:
                blk[:] = NEG
            elif qq == i:
                blk[:] = cm
            else:
                blk[:] = 0.0
    return cstf, cstb


def _prep_weights(inp, L):
    f = np.float32
    swap = np.arange(512).reshape(8, 2, 32)[:, ::-1, :].reshape(512)
    wbig = np.zeros((L, NSLOT, 128, 4096), f)
    wsm = np.zeros((L, NSM, 128, 1280), f)
    par = np.zeros((128, L, NPAR), f)

    def slot_k8(w):
        return w.reshape(8, 128, 512).transpose(1, 0, 2).reshape(128, 4096)

    for l in range(L):
        w_in = inp['w_in'][l]
        b_in = inp['b_in'][l]
        cols = []
        cols.append(np.concatenate([np.arange(1536, 1792), np.arange(1792, 2048)]))
        cols.append(np.concatenate([np.arange(2048, 2304), np.arange(2304, 2560)]))
        cols.append(np.concatenate([np.arange(2816, 3072), np.arange(2560, 2816)]))
        cols.append(np.arange(1024, 1536))
        for c in range(4):
            qc = np.arange(c * 128, (c + 1) * 128)
            cols.append(np.concatenate([qc, swap[qc], 512 + qc, 512 + swap[qc]]))
        for m in range(8):
            cols.append(np.concatenate([3072 + i * 1024 + np.arange(m * 128, (m + 1) * 128) for i in range(4)]))
        for s, cs in enumerate(cols):
            wbig[l, s] = slot_k8(w_in[:, cs])
            par[:, l, s * 4:(s + 1) * 4] = b_in[cs].reshape(4, 128).T
        wo = inp['w_o'][l]
        for hk in range(2):
            wbig[l, 16 + hk] = wo[hk * 512:(hk + 1) * 512].reshape(4, 128, 1024).transpose(1, 0, 2).reshape(128, 4096)
        w1 = inp['w_mlp1'][l]
        w2 = inp['w_mlp2'][l]
        for g in range(4):
            for s in range(2):
                c0 = (2 * g + s) * 512
                wbig[l, 18 + 4 * g + s] = slot_k8(w1[:, c0:c0 + 512])
            for hf in range(2):
                blk = w2[g * 1024:(g + 1) * 1024, hf * 512:(hf + 1) * 512]
                wbig[l, 18 + 4 * g + 2 + hf] = slot_k8(blk)
        wp = inp['w_pool'][l]
        bd = np.zeros((2, 128, 128), f)
        for g in range(4):
            cc, hh = g // 2, g % 2
            bd[cc, hh * 64:(hh + 1) * 64, hh * 64:(hh + 1) * 64] = wp[g]
        wsm[l, 0, :, 0:256] = bd.transpose(1, 0, 2).reshape(128, 256)
        wpall = np.concatenate([inp['w_pa'][l], inp['w_pb'][l], inp['w_pc'][l], inp['w_pd'][l]], 0)
        for m in range(8):
            wsm[l, 1 + m] = wpall[:, m * 128:(m + 1) * 128].reshape(10, 128, 128).transpose(1, 0, 2).reshape(128, 1280)
        wc = inp['w_dw_conf'][l]
        for cc in range(2):
            par[:, l, 64 + cc * 31: 64 + (cc + 1) * 31] = wc[:, cc * 128:(cc + 1) * 128].T
            par[:, l, 126 + cc] = inp['b_dw_conf'][l][cc * 128:(cc + 1) * 128]
            par[:, l, 128 + cc] = inp['ln_conf_g'][l][cc * 128:(cc + 1) * 128]
            par[:, l, 130 + cc] = inp['ln_conf_b'][l][cc * 128:(cc + 1) * 128]
            par[:, l, 132 + cc] = inp['pool_scale'][l][cc * 128:(cc + 1) * 128]
            par[:, l, 134 + cc * 3: 134 + cc * 3 + 3] = inp['w_sc'][l][:, cc * 128:(cc + 1) * 128].T
        par[:, l, 140:148] = inp['ln1_g'][l].reshape(8, 128).T
        par[:, l, 148:156] = inp['ln1_b'][l].reshape(8, 128).T
        par[:, l, 156:164] = inp['ln2_g'][l].reshape(8, 128).T
        par[:, l, 164:172] = inp['ln2_b'][l].reshape(8, 128).T
    return wbig, wsm, np.ascontiguousarray(par.reshape(128, L * NPAR))


_CACHE = {}


def run(inputs, n_layers=DEPTH, cores=NCORES, debug=False):
    inp = {k: np.asarray(v, dtype=np.float32) for k, v in inputs.items()}
    key = (n_layers, debug)
    nc = build_program(n_layers, debug)
    wbig, wsm, par = _prep_weights(inp, n_layers)
    cstf, cstb = _constants()
    x = inp['x']
    in_maps = []
    for c in range(cores):
        in_maps.append({"xT": np.ascontiguousarray(x[c].T), "wbig": wbig, "wsm": wsm, "par": par,
                        "cstf": cstf, "cstb": cstb})
    res = run_bass_kernel_spmd(nc, in_maps, core_ids=list(range(cores)))
    outs = [np.ascontiguousarray(r["outT"].T) for r in res.results]
    return np.stack(outs, 0), res


def kernel(**inputs):
    out, _ = run(inputs)
    return out.astype(np.float32)
```
